# Optimizing a Trainium2 kernel written in Bass

```python
import math
import jax, jax.numpy as jnp
from jax import lax
import numpy as np

D_MODEL = 2048
BATCH = 2
SEQ = 8192
DEPTH = 1

RET_HEADS = 8
RET_DK = 128
RET_DV = 128
RET_CHUNK = 128
ROPE_BASE = 10000.0
SWA_HEADS = 16
SWA_KV_HEADS = 4
SWA_HD = 64
SWA_WINDOW = 128
SWA_BLOCK = 128
REL_BUCKETS = 32
REL_MAX_DIST = 128
MEM_LEN = 256
XA_HEADS = 4
XA_HD = D_MODEL // XA_HEADS
D_FF = 5632
CONV_W = 3
LN_EPS = 1e-5
DN_ALPHA = (2 * DEPTH) ** 0.25
DN_BETA = (8 * DEPTH) ** -0.25

RET_W = RET_HEADS * RET_DV
SWA_W = SWA_HEADS * SWA_HD
MIX_W = RET_W + SWA_W
SWA_KV_W = SWA_KV_HEADS * SWA_HD
SPLIT_SIZES = (RET_HEADS * RET_DK, RET_HEADS * RET_DK, RET_W, RET_W, SWA_W, SWA_KV_W, SWA_KV_W)
IN_W = sum(SPLIT_SIZES)

kernel_name = "hybrid_retention_swa_sink_deepnorm_layer"


def layer_norm(x, g, b):
    xf = x.astype(jnp.float32)
    mu = jnp.mean(xf, axis=-1, keepdims=True)
    var = jnp.mean(jnp.square(xf - mu), axis=-1, keepdims=True)
    y = (xf - mu) * lax.rsqrt(var + LN_EPS)
    return (y * g.astype(jnp.float32) + b.astype(jnp.float32)).astype(x.dtype)


def rotate(x, pos):
    half = x.shape[-1] // 2
    inv = 1.0 / (ROPE_BASE ** (jnp.arange(half, dtype=jnp.float32) / half))
    ang = pos.astype(jnp.float32)[:, None] * inv[None, :]
    cos = jnp.cos(ang)[None, :, None, :]
    sin = jnp.sin(ang)[None, :, None, :]
    x1, x2 = x[..., :half], x[..., half:]
    return jnp.concatenate([x1 * cos - x2 * sin, x1 * sin + x2 * cos], axis=-1)


def retention(q, k, v):
    B, S, H, dk = q.shape
    dv = v.shape[-1]
    C = RET_CHUNK
    NC = S // C
    log_gamma = jnp.log1p(-jnp.exp2(-5.0 - jnp.arange(H, dtype=jnp.float32)))
    idx = jnp.arange(C, dtype=jnp.float32)
    diff = idx[:, None] - idx[None, :]
    dmat = jnp.where(diff[None] >= 0,
                     jnp.exp(log_gamma[:, None, None] * jnp.maximum(diff, 0.0)[None]),
                     0.0)
    qc = q.reshape(B, NC, C, H, dk)
    kc = k.reshape(B, NC, C, H, dk)
    vc = v.reshape(B, NC, C, H, dv)
    scores = jnp.einsum('bnihd,bnjhd->bnhij', qc, kc) * dmat
    o_intra = jnp.einsum('bnhij,bnjhe->bnihe', scores, vc)
    k_decay = jnp.exp(log_gamma[:, None] * (C - 1 - idx)[None, :])
    kv = jnp.einsum('bnjhd,hj,bnjhe->bnhde', kc, k_decay, vc)
    chunk_decay = jnp.exp(log_gamma * C)[:, None, None]

    def step(state, kv_n):
        return state * chunk_decay + kv_n, state

    _, prev = lax.scan(step, jnp.zeros((B, H, dk, dv), jnp.float32), jnp.moveaxis(kv, 1, 0))
    prev = jnp.moveaxis(prev, 0, 1)
    q_decay = jnp.exp(log_gamma[:, None] * (idx + 1.0)[None, :])
    o_cross = jnp.einsum('bnihd,bnhde->bnihe', qc, prev) * q_decay.T[None, None, :, :, None]
    return (o_intra + o_cross).reshape(B, S, H, dv)


def t5_bucket(n):
    max_exact = REL_BUCKETS // 2
    nf = jnp.maximum(n, 1).astype(jnp.float32)
    large = max_exact + (jnp.log(nf / max_exact) / math.log(REL_MAX_DIST / max_exact)
                         * (REL_BUCKETS - max_exact)).astype(jnp.int32)
    large = jnp.minimum(large, REL_BUCKETS - 1)
    return jnp.where(n < max_exact, n, large)


def swa_sink_attention(q, k, v, sinks, rel_bias):
    B, S, Hq, d = q.shape
    Hkv = k.shape[2]
    G = Hq // Hkv
    L = SWA_BLOCK
    NB = S // L
    qb = q.reshape(B, NB, L, Hkv, G, d)

    def band(t):
        tb = t.reshape(B, NB, L, Hkv, d)
        prev = jnp.pad(tb, ((0, 0), (1, 0), (0, 0), (0, 0), (0, 0)))[:, :NB]
        return jnp.concatenate([prev, tb], axis=2)

    kb, vb = band(k), band(v)
    logits = jnp.einsum('bnikgd,bnjkd->bnkgij', qb, kb).astype(jnp.float32) * (d ** -0.5)
    i = jnp.arange(L)[:, None]
    j = jnp.arange(2 * L)[None, :]
    dist = i + L - j
    blk = jnp.arange(NB)[:, None, None]
    valid = (dist >= 0) & (dist < SWA_WINDOW) & (blk * L + j - L >= 0)
    bias = rel_bias.astype(jnp.float32)[t5_bucket(jnp.maximum(dist, 0))]
    bias = jnp.transpose(bias, (2, 0, 1)).reshape(Hkv, G, L, 2 * L)
    logits = jnp.where(valid[None, :, None, None], logits + bias[None, None], -jnp.inf)
    sink = sinks.astype(jnp.float32).reshape(Hkv, G)[None, None, :, :, None, None]
    m = jnp.maximum(jnp.max(logits, axis=-1, keepdims=True), sink)
    p = jnp.exp(logits - m)
    p = p / (jnp.sum(p, axis=-1, keepdims=True) + jnp.exp(sink - m))
    out = jnp.einsum('bnkgij,bnjkd->bnikgd', p.astype(vb.dtype), vb)
    return out.reshape(B, S, Hq * d)


def memory_cross_attention(x, mem, wq, wkv, wo):
    B, S, _ = x.shape
    q = (x @ wq).reshape(B, S, XA_HEADS, XA_HD)
    kv = mem @ wkv
    k, v = jnp.split(kv, 2, axis=-1)
    k = k.reshape(B, -1, XA_HEADS, XA_HD)
    v = v.reshape(B, -1, XA_HEADS, XA_HD)
    logits = jnp.einsum('bshd,bmhd->bhsm', q, k).astype(jnp.float32) * (XA_HD ** -0.5)
    p = jax.nn.softmax(logits, axis=-1).astype(v.dtype)
    o = jnp.einsum('bhsm,bmhd->bshd', p, v).reshape(B, S, D_MODEL)
    return o @ wo


def conv_ffn(x, w_up, conv_w, conv_b, w_down):
    S = x.shape[1]
    u, g = jnp.split(x @ w_up, 2, axis=-1)
    gp = jnp.pad(g, ((0, 0), (CONV_W - 1, 0), (0, 0)))
    gc = conv_b + sum(gp[:, tap:tap + S] * conv_w[tap] for tap in range(CONV_W))
    return (jax.nn.silu(gc) * u) @ w_down


def setup_inputs(seed: int = 0) -> dict:
    key = jax.random.key(seed)
    ks = jax.random.split(key, 24)
    f32 = jnp.float32

    def nrm(k, shape, scale):
        return jax.random.normal(k, shape, f32) * scale

    col_scale = jnp.concatenate([
        jnp.full((SPLIT_SIZES[0],), 1.0, f32), jnp.full((SPLIT_SIZES[1],), 1.0, f32),
        jnp.full((SPLIT_SIZES[2],), DN_BETA, f32), jnp.full((SPLIT_SIZES[3],), 1.0, f32),
        jnp.full((SPLIT_SIZES[4],), 1.0, f32), jnp.full((SPLIT_SIZES[5],), 1.0, f32),
        jnp.full((SPLIT_SIZES[6],), DN_BETA, f32)])
    xa_kv_scale = jnp.concatenate([jnp.ones((D_MODEL,), f32), jnp.full((D_MODEL,), DN_BETA, f32)])
    return {
        "x": nrm(ks[0], (BATCH, SEQ, D_MODEL), 1.0),
        "mem": nrm(ks[1], (BATCH, MEM_LEN, D_MODEL), 1.0),
        "w_in": nrm(ks[2], (DEPTH, D_MODEL, IN_W), D_MODEL ** -0.5) * col_scale,
        "ret_gn_g": 1.0 + nrm(ks[3], (DEPTH, RET_W), 0.01),
        "swa_sinks": nrm(ks[4], (DEPTH, SWA_HEADS), 0.5),
        "rel_bias": nrm(ks[5], (REL_BUCKETS, SWA_HEADS), 0.5),
        "w_o": nrm(ks[6], (DEPTH, MIX_W, D_MODEL), MIX_W ** -0.5 * DN_BETA),
        "ln1_g": 1.0 + nrm(ks[7], (DEPTH, D_MODEL), 0.01),
        "ln1_b": nrm(ks[8], (DEPTH, D_MODEL), 0.01),
        "xa_wq": nrm(ks[9], (DEPTH, D_MODEL, D_MODEL), D_MODEL ** -0.5),
        "xa_wkv": nrm(ks[10], (DEPTH, D_MODEL, 2 * D_MODEL), D_MODEL ** -0.5) * xa_kv_scale,
        "xa_wo": nrm(ks[11], (DEPTH, D_MODEL, D_MODEL), D_MODEL ** -0.5 * DN_BETA),
        "ln2_g": 1.0 + nrm(ks[12], (DEPTH, D_MODEL), 0.01),
        "ln2_b": nrm(ks[13], (DEPTH, D_MODEL), 0.01),
        "ffn_w_up": nrm(ks[14], (DEPTH, D_MODEL, 2 * D_FF), D_MODEL ** -0.5 * DN_BETA),
        "ffn_conv_w": nrm(ks[15], (DEPTH, CONV_W, D_FF), CONV_W ** -0.5),
        "ffn_conv_b": nrm(ks[16], (DEPTH, D_FF), 0.01),
        "ffn_w_down": nrm(ks[17], (DEPTH, D_FF, D_MODEL), D_FF ** -0.5 * DN_BETA),
        "ln3_g": 1.0 + nrm(ks[18], (DEPTH, D_MODEL), 0.01),
        "ln3_b": nrm(ks[19], (DEPTH, D_MODEL), 0.01),
    }


def reference(x, mem, w_in, ret_gn_g, swa_sinks, rel_bias, w_o, ln1_g, ln1_b,
              xa_wq, xa_wkv, xa_wo, ln2_g, ln2_b,
              ffn_w_up, ffn_conv_w, ffn_conv_b, ffn_w_down, ln3_g, ln3_b):
    B, S, _ = x.shape
    pos = jnp.arange(S)
    offsets = [int(o) for o in np.cumsum(SPLIT_SIZES)[:-1]]
    for l in range(DEPTH):
        proj = x @ w_in[l]
        q_r, k_r, v_r, g_r, q_s, k_s, v_s = jnp.split(proj, offsets, axis=-1)
        qr = rotate(q_r.astype(jnp.float32).reshape(B, S, RET_HEADS, RET_DK), pos)
        kr = rotate(k_r.astype(jnp.float32).reshape(B, S, RET_HEADS, RET_DK), pos) * (RET_DK ** -0.5)
        vr = v_r.astype(jnp.float32).reshape(B, S, RET_HEADS, RET_DV)
        o_r = retention(qr, kr, vr)
        mu = jnp.mean(o_r, axis=-1, keepdims=True)
        var = jnp.mean(jnp.square(o_r - mu), axis=-1, keepdims=True)
        o_r = ((o_r - mu) * lax.rsqrt(var + LN_EPS)).reshape(B, S, RET_W) * ret_gn_g[l].astype(jnp.float32)
        o_r = (jax.nn.silu(g_r.astype(jnp.float32)) * o_r).astype(x.dtype)
        o_s = swa_sink_attention(q_s.reshape(B, S, SWA_HEADS, SWA_HD),
                                 k_s.reshape(B, S, SWA_KV_HEADS, SWA_HD),
                                 v_s.reshape(B, S, SWA_KV_HEADS, SWA_HD),
                                 swa_sinks[l], rel_bias).astype(x.dtype)
        mix = jnp.concatenate([o_r, o_s], axis=-1) @ w_o[l]
        x = layer_norm(DN_ALPHA * x + mix, ln1_g[l], ln1_b[l])
        xa = memory_cross_attention(x, mem, xa_wq[l], xa_wkv[l], xa_wo[l])
        x = layer_norm(DN_ALPHA * x + xa, ln2_g[l], ln2_b[l])
        ff = conv_ffn(x, ffn_w_up[l], ffn_conv_w[l], ffn_conv_b[l], ffn_w_down[l])
        x = layer_norm(DN_ALPHA * x + ff, ln3_g[l], ln3_b[l])
    return x
```

```python
import bisect
import math
from contextlib import ExitStack
import numpy as np
import concourse.bass as bass
import concourse.mybir as mybir
from concourse.bass_utils import run_bass_kernel_spmd

F32 = mybir.dt.float32
BF16 = mybir.dt.bfloat16
AF = mybir.ActivationFunctionType
ALU = mybir.AluOpType
class Reg:
    __slots__ = ("name", "w", "rc", "rd", "excl")

    def __init__(self, name, excl=False):
        self.name = name
        self.excl = excl
        self.w = None
        self.rc = {}
        self.rd = {}


class Prog:
    ENG = ["pe", "act", "dve", "pool", "sp"]

    def __init__(self):
        self.ops = {e: [] for e in self.ENG}
        self.marks = {e: [] for e in self.ENG}
        self.known = {e: {} for e in self.ENG}
        self.dcount = {}
        self.bar = {}
        self.stopped = False

    def _ctoken(self, eng, seq):
        marks = self.marks[eng]
        i = bisect.bisect_left(marks, seq)
        if i == len(marks):
            last = len(self.ops[eng]) - 1
            assert last >= seq
            marks.append(last)
            self.ops[eng][last]["inc"] = True
            i = len(marks) - 1
        return (("e", eng), i + 1)

    def barrier(self):
        if self.stopped:
            return
        toks = {}
        for e in self.ENG:
            if e == "sp" or not self.ops[e]:
                continue
            k, v = self._ctoken(e, len(self.ops[e]) - 1)
            toks[k] = v
        for k, c in self.dcount.items():
            toks[("d", k)] = 16 * c
        for e in self.ENG:
            self.bar[e] = dict(toks)

    def op(self, eng, fn, reads=(), writes=(), dkey=None, mark=None):
        if self.stopped and fn is not None:
            return None
        ex = [r for r in reads if r.excl]
        if ex:
            reads = [r for r in reads if not r.excl]
            writes = list(writes) + [r for r in ex if r not in writes]
        deps = []
        for r in reads:
            if r.w is not None:
                deps.append(r.w)
        for w in writes:
            if w.w is not None:
                deps.append(w.w)
            for e, sq in w.rc.items():
                deps.append(("c", e, sq))
            for k, v in w.rd.items():
                deps.append(("d", k, v))
        waits = {}
        for tok in deps:
            if tok[0] == "c":
                if tok[1] == eng and eng == "pe":
                    continue
                semk, val = self._ctoken(tok[1], tok[2])
            else:
                semk, val = ("d", tok[1]), tok[2]
            if self.known[eng].get(semk, 0) >= val:
                continue
            if waits.get(semk, 0) < val:
                waits[semk] = val
        if self.bar.get(eng):
            for semk, val in self.bar[eng].items():
                if semk == ("e", eng) and eng == "pe":
                    continue
                if self.known[eng].get(semk, 0) >= val:
                    continue
                if waits.get(semk, 0) < val:
                    waits[semk] = val
            self.bar[eng] = None
        for k, v in waits.items():
            self.known[eng][k] = v
        seq = len(self.ops[eng])
        rec = dict(fn=fn, waits=list(waits.items()), inc=False, dkey=dkey)
        self.ops[eng].append(rec)
        if dkey is not None:
            c = self.dcount.get(dkey, 0) + 1
            self.dcount[dkey] = c
            tok = ("d", dkey, 16 * c)
        else:
            tok = ("c", eng, seq)
            if (mark if mark is not None else (eng != "pe")):
                self.marks[eng].append(seq)
                rec["inc"] = True
        for r in reads:
            if tok[0] == "c":
                if r.rc.get(eng, -1) < seq:
                    r.rc[eng] = seq
            else:
                r.rd[dkey] = tok[2]
        for w in writes:
            w.w = tok
            w.rc = {}
            w.rd = {}
        return tok

    def emit(self, nc, stack):
        keys = set()
        for e in self.ENG:
            for rec in self.ops[e]:
                for k, _ in rec["waits"]:
                    keys.add(k)
                if rec["dkey"] is not None:
                    keys.add(("d", rec["dkey"]))
                elif rec["inc"]:
                    keys.add(("e", e))
        sems = {}
        for k in sorted(keys, key=str):
            sems[k] = stack.enter_context(nc.semaphore("s_" + "_".join(str(x) for x in k)))
        block = stack.enter_context(nc.Block())

        def run(engname):
            def f(eng):
                for rec in self.ops[engname]:
                    for k, v in rec["waits"]:
                        eng.wait_ge(sems[k], v)
                    if rec["fn"] is None:
                        continue
                    ins = rec["fn"](eng)
                    if rec["dkey"] is not None:
                        ins.then_inc(sems[("d", rec["dkey"])], 16)
                    elif rec["inc"]:
                        ins.then_inc(sems[("e", engname)], 1)
            return f

        block.tensor(run("pe"))
        block.scalar(run("act"))
        block.vector(run("dve"))
        block.gpsimd(run("pool"))
        block.sync(run("sp"))
        return len(sems)

D = 2048
KC = 16
NOWN = 16
NPRE = 48
GROUPS = [[-1, 0, 1, 2, 3, 4], [5, 6, 7, 8, 9, 10], [11, 12, 13, 14, 15]]
MAXT = 7


def set_cfg(nown, groups):
    global NOWN, NPRE, GROUPS
    NOWN, NPRE, GROUPS = nown, 3 * nown, groups


def nxc():
    return NPRE + 1 + NOWN
ALPHA = 2.0 ** 0.25
EPS = 1e-5
FF = 5632
NF = 44
NEG = -30000.0


def xc_tile(gt):
    return NPRE + 1 + gt


def colgroups(ntok, mx=512):
    out = []
    o = 0
    while o < ntok:
        n = min(mx, ntok - o)
        out.append((o, n))
        o += n
    return out


class Pool:
    def __init__(self, aps, name, excl=False):
        self.aps = aps
        self.regs = [Reg("%s%d" % (name, i), excl) for i in range(len(aps))]
        self.i = 0

    def next(self):
        j = self.i % len(self.aps)
        self.i += 1
        return self.aps[j], self.regs[j]


def build():
    import os
    nc = bass.Bass("TRN2", target_bir_lowering=False)
    P = Prog()
    kstop = int(os.environ.get("KSTOP", "-1"))
    cnt = [0]

    kstopn = os.environ.get("KSTOPN", "")

    def chk(name):
        cnt[0] += 1
        if cnt[0] == kstop or (kstopn and name == kstopn and not P.stopped):
            print("STOP at checkpoint", cnt[0], name, flush=True)
            P.stopped = True

    def din(name, shape, dt=F32):
        return nc.dram_tensor(name, list(shape), dt, kind="ExternalInput").ap()

    xc = din("xc", [nxc() * 128, D])
    memd = din("memb", [256, D])
    w_r = din("w_r", [16, 128, KC, 256])
    w_s = din("w_s", [7, 128, KC, 256])
    w_o = din("w_o", [8, 128, KC, 256])
    w_q = din("w_q", [8, 128, KC, 256])
    w_kv = din("w_kv", [16, 128, KC, 256])
    w_xo = din("w_xo", [8, 128, KC, 256])
    w_up = din("w_up", [NF, 128, KC, 256])
    w_dn = din("w_dn", [4, 8, 128, 11, 256])
    cs_pre = din("cs_pre", [128, NPRE, 2, 64])
    cs_main = din("cs_main", [128, 1 + NOWN, 2, 64])
    kdec_pre = din("kdec_pre", [128, NPRE, 8])
    qkdec_d = din("qkdec", [128, 2, 8])
    mask_d = din("mask01", [128, 128])
    ident_d = din("ident", [128, 128])
    biasT_d = din("biasT", [128, 2, 16, 128])
    sink_d = din("sinkrep", [128, 8])
    flag_d = din("flag", [128, 1])
    gng_d = din("gng", [1024])
    lnp_d = din("lnp", [6, D])
    convp_d = din("convp", [128, NF, 4])
    gamc_d = din("gamc", [128, 8])
    y = nc.dram_tensor("y", [NOWN * 128, D], F32, kind="ExternalOutput").ap()

    st = ExitStack()
    with st:
        def sb(name, shape, dt):
            return st.enter_context(nc.sbuf_tensor(name, list(shape), dt))

        NX = KC * 8 * 128
        NM = KC * MAXT * 128
        NR = MAXT * D * 2
        big = sb("big", [128, NX + NM + NR], BF16)
        Xf = big[:, 0:NX]
        X = Xf.rearrange("p (k t) -> p k t", k=KC)
        Mf = big[:, NX:NX + NM]
        M = Mf.rearrange("p (k t) -> p k t", k=KC)
        Rb = big[:, NX + NM:NX + NM + NR]
        R = Rb.bitcast(F32).rearrange("p (s d) -> p s d", s=MAXT)
        regX = [Reg("X%d" % i) for i in range(8)]
        regM = [Reg("M%d" % i) for i in range(MAXT)]
        regR = [Reg("R%d" % i) for i in range(MAXT)]

        NWS = 4
        Wt = sb("wslots", [128, NWS, KC * 256], BF16)
        Wsl = Pool([Wt[:, i, :].rearrange("p (k c) -> p k c", k=KC) for i in range(NWS)], "W")
        stg_t = sb("stg", [128, 2, 4 * 256], F32)
        STG = Pool([stg_t[:, i, :].rearrange("p (k c) -> p k c", k=4) for i in range(2)], "STG")
        tf_t = sb("tf", [128, 6, 520], F32)
        TF = Pool([tf_t[:, i, :] for i in range(6)], "tf")
        tb_t = sb("tb", [128, 8, 512], BF16)
        TB = Pool([tb_t[:, i, :] for i in range(8)], "tb")
        sm_t = sb("sm", [128, 6, 32], F32)
        SM = Pool([sm_t[:, i, :] for i in range(6)], "sm")

        ident = sb("identb", [128, 128], BF16)
        ones = sb("onesb", [128, 128], BF16)
        mask01 = sb("mask01s", [128, 128], F32)
        Wst = sb("wstate", [128, 8, 128], F32)
        Sbf = sb("sbf", [128, 8, 128], BF16)
        prevK = sb("prevK", [128, 4, 128], BF16)
        prevV = sb("prevV", [128, 4, 2, 64], BF16)
        esink = sb("esink", [128, 8], F32)
        qkdec = sb("qkdecs", [128, 2, 8], F32)
        gamc = sb("gamcs", [128, 8], F32)
        flag = sb("flags", [128, 1], F32)
        convp = sb("convps", [128, NF, 4], F32)
        gtail = sb("gtail", [128, NF, 2], F32)
        memKT = sb("memKT", [128, KC, 256], BF16)
        memV = sb("memV", [128, 2, D], BF16)
        rconst = Reg("const")
        rWst = [Reg("Wst%d" % h) for h in range(8)]
        rSbf = [Reg("Sbf%d" % h) for h in range(8)]
        rprev = Reg("prevKV")
        rgtail = Reg("gtail")
        rmem = Reg("memKV")

        pmm_t = [st.enter_context(nc.psum_tensor("pmm%d" % i, [128, 512], F32)) for i in range(4)]
        PMM = Pool([t[:] for t in pmm_t], "pmm", excl=True)
        ptr_t = [st.enter_context(nc.psum_tensor("ptr%d" % i, [128, 1024], BF16)) for i in range(2)]
        PTR = Pool([t[:] for t in ptr_t], "ptr", excl=True)
        pax_t = [st.enter_context(nc.psum_tensor("pax%d" % i, [128, 512], F32)) for i in range(2)]
        PAX = Pool([t[:] for t in pax_t], "pax", excl=True)
        PTRF = Pool([t[:].bitcast(F32) for t in ptr_t], "ptrf", excl=True)
        PTRF.regs = PTR.regs

        dctr = [0]

        def DMA(out, in_, reads, writes, key):
            P.op("sp", lambda e: e.dma_start(out=out, in_=in_), reads=reads, writes=writes, dkey=key)

        def MMg(out, pairs, reads, wreg):
            n = len(pairs)
            for i, (l, r) in enumerate(pairs):
                P.op("pe", lambda e, l=l, r=r, i=i: e.matmul(out, lhsT=l, rhs=r, start=(i == 0), stop=(i == n - 1)),
                     reads=reads, writes=[wreg], mark=(i == n - 1))

        def TR(out, in_, reads, wreg, mark):
            P.op("pe", lambda e: e.transpose(out=out, in_=in_, identity=ident[:]), reads=reads + [rconst], writes=[wreg], mark=mark)

        def ACT(out, in_, func, reads, writes, scale=1.0, bias=0.0):
            P.op("act", lambda e: e.activation(out=out, in_=in_, func=func, scale=scale, bias=bias), reads=reads, writes=writes)

        def CP(eng, out, in_, reads, writes):
            if eng == "act":
                ACT(out, in_, AF.Copy, reads, writes)
            else:
                P.op(eng, lambda e: e.tensor_copy(out=out, in_=in_), reads=reads, writes=writes)

        def TT(out, a, b, op, reads, writes, eng="dve"):
            P.op(eng, lambda e: e.tensor_tensor(out=out, in0=a, in1=b, op=op), reads=reads, writes=writes)

        def TS(out, a, s1, s2, op0, op1, reads, writes):
            if op1 is None:
                P.op("dve", lambda e: e.tensor_scalar(out=out, in0=a, scalar1=s1, scalar2=None, op0=op0), reads=reads, writes=writes)
            else:
                P.op("dve", lambda e: e.tensor_scalar(out=out, in0=a, scalar1=s1, scalar2=s2, op0=op0, op1=op1), reads=reads, writes=writes)

        def STT(out, a, s, b, op0, op1, reads, writes):
            P.op("dve", lambda e: e.scalar_tensor_tensor(out=out, in0=a, scalar=s, in1=b, op0=op0, op1=op1), reads=reads, writes=writes)

        castctr = [0]

        def stage_dma(slot_i, src, nkc, C):
            sap, sreg = STG.aps[slot_i], STG.regs[slot_i]
            DMA(sap[:, 0:nkc, 0:C], src, [], [sreg], "stg%d" % slot_i)

        def stage_cast(slot_i, dst, nkc, C, dreg):
            sap, sreg = STG.aps[slot_i], STG.regs[slot_i]
            castctr[0] += 1
            if castctr[0] % 2 == 0:
                ACT(dst, sap[:, 0:nkc, 0:C], AF.Copy, [sreg], [dreg])
            else:
                P.op("dve", lambda e: e.tensor_copy(out=dst, in_=sap[:, 0:nkc, 0:C]), reads=[sreg], writes=[dreg])

        class WQ:
            def __init__(self):
                self.plan = []
                self.slots = []
                self.pos = 0
                self.chunks = []
                self.nd = 0
                self.ncast = 0
                self.blk_end = []

            def add(self, src_fn, nk=KC, C=256, dst=None, dreg=None):
                self.plan.append((src_fn, nk, C, dst, dreg))

            def _schedule(self):
                i = len(self.slots)
                fn, nk, C, dst, dreg = self.plan[i]
                if dst is None:
                    dst, dreg = Wsl.next()
                self.slots.append((dst, dreg))
                k0 = 0
                while k0 < nk:
                    k1 = min(nk, k0 + 4)
                    self.chunks.append((i, dst[:, k0:k1, 0:C], fn(k0, k1), k1 - k0, C, dreg))
                    k0 = k1
                self.blk_end.append(len(self.chunks))

            def _dma(self):
                c = self.chunks[self.nd]
                stage_dma(self.nd % 2, c[2], c[3], c[4])
                self.nd += 1

            def _pump1(self):
                self.pump(1)

            def pump(self, n=1):
                for _ in range(n):
                    if self.ncast >= len(self.chunks):
                        return
                    while self.nd <= self.ncast:
                        self._dma()
                    c = self.chunks[self.ncast]
                    stage_cast(self.ncast % 2, c[1], c[3], c[4], c[5])
                    self.ncast += 1
                    if self.nd < len(self.chunks) and self.nd < self.ncast + 1:
                        self._dma()

            def get(self, ahead=2):
                while len(self.slots) <= min(self.pos + ahead, len(self.plan) - 1):
                    self._schedule()
                while self.ncast < self.blk_end[self.pos]:
                    self._pump1()
                r = self.slots[self.pos]
                self.pos += 1
                return r

        bst_t = sb("bst", [128, 2, 4 * 256], BF16)
        Wflat = Wt[:].rearrange("p s c -> p (s c)")
        CIN = Pool([stg_t[:, i, :].rearrange("p (k c) -> p k c", k=4) for i in range(2)] +
                   [Wflat[:, i * 2048:(i + 1) * 2048].bitcast(F32).rearrange("p (k c) -> p k c", k=4) for i in range(4)], "CIN")
        CIN.regs[0], CIN.regs[1] = STG.regs[0], STG.regs[1]
        COUT = Pool([bst_t[:, i, :].rearrange("p (k c) -> p k c", k=4) for i in range(2)] +
                    [Wflat[:, 8192 + i * 1024:8192 + (i + 1) * 1024].rearrange("p (k c) -> p k c", k=4) for i in range(4)], "COUT")
        RING = 6

        class Conv:
            def __init__(self):
                self.chunks = []
                self.i = 0
                self.nin = 0
                self.ring = RING
                self.limit = None
                self.end_of = {}

            def add_block(self, src_fn, dst_fn, nk, reg):
                k0 = 0
                while k0 < nk:
                    k1 = min(nk, k0 + 4)
                    self.chunks.append((src_fn(k0, k1), dst_fn(k0, k1), k1 - k0, reg))
                    k0 = k1
                self.end_of[id(reg)] = len(self.chunks)

            def ensure(self, reg):
                while self.i < self.end_of[id(reg)]:
                    self.pump(1)

            def _din(self, k):
                src, dst, nkc, reg = self.chunks[k]
                si = k % self.ring
                DMA(CIN.aps[si][:, 0:nkc, :], src, [], [CIN.regs[si]], "stg%d" % si if si < 2 else "cin%d" % si)
                self.nin = k + 1

            def pump(self, n=1):
                for _ in range(n):
                    lim = len(self.chunks) if self.limit is None else self.limit
                    if self.i >= lim:
                        return
                    i = self.i
                    while self.nin < min(i + self.ring, lim):
                        self._din(self.nin)
                    src, dst, nkc, reg = self.chunks[i]
                    si = i % self.ring
                    castctr[0] += 1
                    cin, cout = CIN.aps[si][:, 0:nkc, :], COUT.aps[si][:, 0:nkc, :]
                    ACT(cout, cin, AF.Copy, [CIN.regs[si]], [COUT.regs[si]])
                    DMA(dst, cout, [COUT.regs[si]], [reg], "cout%d" % si)
                    self.i += 1

            def flush(self):
                lim = len(self.chunks) if self.limit is None else self.limit
                while self.i < lim:
                    self.pump(1)

        conv = Conv()
        scr = {}
        sregs = {}

        def mkscr(name, wt, nblk, nk=KC):
            scr[name] = nc.dram_tensor("scr_" + name, [nblk, 128, nk, 256], BF16).ap()
            sregs[name] = [Reg("scr_%s%d" % (name, b)) for b in range(nblk)]

        mkscr("w_r", w_r, 16); mkscr("w_s", w_s, 7); mkscr("w_o", w_o, 8); mkscr("w_q", w_q, 8); mkscr("w_xo", w_xo, 8)
        mkscr("w_up", w_up, NF); mkscr("w_dn", w_dn, 32, 11)

        def conv_add(name, wt, b):
            conv.add_block(lambda k0, k1: wt[b, :, k0:k1, :], lambda k0, k1: scr[name][b, :, k0:k1, :], KC, sregs[name][b])

        for b in range(16):
            conv_add("w_r", w_r, b)
        for b in (4, 5, 6, 0, 1, 2, 3):
            conv_add("w_s", w_s, b)
        for b in range(8):
            conv_add("w_o", w_o, b)
        for b in range(8):
            conv_add("w_q", w_q, b)
        for b in range(8):
            conv_add("w_xo", w_xo, b)
        conv.limit = len(conv.chunks)
        for fb in range(4):
            for fi in range(11):
                conv_add("w_up", w_up, fb * 11 + fi)
            for cb in range(8):
                conv.add_block((lambda k0, k1, fb=fb, cb=cb: w_dn[fb, cb, :, k0:k1, :]),
                               (lambda k0, k1, fb=fb, cb=cb: scr["w_dn"][fb * 8 + cb, :, k0:k1, :]), 11, sregs["w_dn"][fb * 8 + cb])

        class WQD:
            def __init__(self):
                self.plan = []
                self.slots = []
                self.pos = 0

            def add(self, name, b, nk=KC):
                self.plan.append((name, b, nk))

            def pump(self, n=1):
                conv.pump(n)

            def get_pair(self):
                assert len(self.slots) == self.pos or len(self.slots) == self.pos + 2, (len(self.slots), self.pos)
                def sched():
                    if Wsl.i % 2 == 1:
                        Wsl.i += 1
                    j = Wsl.i % NWS
                    pair = Wt[:, j:j + 2, :].rearrange("p s c -> p (s c)").rearrange("p (k c) -> p k c", k=KC)
                    regs = [Wsl.regs[j], Wsl.regs[j + 1]]
                    Wsl.i += 2
                    for t in range(2):
                        name, b, nk = self.plan[len(self.slots)]
                        conv.ensure(sregs[name][b])
                        DMA(pair[:, :, t * 256:(t + 1) * 256], scr[name][b], [sregs[name][b]], regs, "wsl%d" % (j + t))
                        self.slots.append((pair, regs))
                if len(self.slots) == self.pos:
                    sched()
                r = self.slots[self.pos]
                self.pos += 2
                if len(self.slots) == self.pos and self.pos + 1 < len(self.plan) and self.plan[self.pos][0] == "w_r":
                    sched()
                return r

            def get(self, ahead=2):
                while len(self.slots) <= min(self.pos + ahead, len(self.plan) - 1):
                    name, b, nk = self.plan[len(self.slots)]
                    conv.ensure(sregs[name][b])
                    j = Wsl.i % NWS
                    dst, dreg = Wsl.next()
                    DMA(dst[:, 0:nk, :], scr[name][b], [sregs[name][b]], [dreg], "wsl%d" % j)
                    self.slots.append((dst, dreg))
                r = self.slots[self.pos]
                self.pos += 1
                return r

        def wsrc(wt, blk):
            return lambda k0, k1: wt[blk, :, k0:k1, :]

        t_, r_ = TF.next()
        DMA(t_[:, 0:128], ident_d, [], [r_], "c0")
        CP("act", ident[:], t_[:, 0:128], [r_], [rconst])
        DMA(mask01[:], mask_d, [], [rconst], "c1")
        DMA(qkdec[:], qkdec_d, [], [rconst], "c2")
        DMA(gamc[:], gamc_d, [], [rconst], "c3")
        DMA(flag[:], flag_d, [], [rconst], "c4")
        DMA(convp[:], convp_d, [], [rconst], "c5")
        DMA(esink[:], sink_d, [], [rconst], "c6")
        ACT(esink[:], esink[:], AF.Exp, [rconst], [rconst])
        P.op("dve", lambda e: e.memset(ones[:], 1.0), writes=[rconst])
        P.op("dve", lambda e: e.memset(gtail[:], 0.0), writes=[rgtail])
        for h in range(8):
            P.op("dve", lambda e, h=h: e.memset(Wst[:, h, :], 0.0), writes=[rWst[h]])

        chk("const")
        def load_xT(src_rows, xf, rxf, xb, rxb, dst3, dreg, key, src_is_sbuf=False, src_reads=(), cast_eng="act", evac=("dve", "act")):
            if not src_is_sbuf:
                DMA(xf, src_rows, [], [rxf], key)
                CP(cast_eng, xb, xf, [rxf], [rxb])
            else:
                CP("act", xb, src_rows, list(src_reads), [rxb])
            for half in range(2):
                pt, rpt = PTR.next()
                for k in range(8):
                    kc = half * 8 + k
                    TR(pt[:, k * 128:(k + 1) * 128], xb[:, kc * 128:(kc + 1) * 128], [rxb], rpt, mark=(k == 7))
                CP(evac[half], dst3[:, half * 8:half * 8 + 8, :],
                   pt.rearrange("p (k t) -> p k t", k=8), [rpt], [dreg])

        def rope2(src3, cs, dec_bc, out_bf, reads, rout):
            ta, ra = TF.next()
            A = ta[:, 0:256].rearrange("p (h c f) -> p h c f", h=2, c=2)
            Bv = ta[:, 256:512].rearrange("p (h c f) -> p h c f", h=2, c=2)
            csb = cs.unsqueeze(1).to_broadcast([128, 2, 2, 64])
            x1 = src3[:, :, 0:64].unsqueeze(2).to_broadcast([128, 2, 2, 64])
            x2 = src3[:, :, 64:128].unsqueeze(2).to_broadcast([128, 2, 2, 64])
            TT(A, x1, csb, ALU.mult, reads, [ra])
            TT(Bv, x2, csb, ALU.mult, reads + [ra], [ra])
            tr_, rr = TF.next()
            kr = tr_[:, 0:256].rearrange("p (h f) -> p h f", h=2)
            TT(kr[:, :, 0:64], A[:, :, 0, :], Bv[:, :, 1, :], ALU.subtract, [ra], [rr])
            TT(kr[:, :, 64:128], A[:, :, 1, :], Bv[:, :, 0, :], ALU.add, [ra, rr], [rr])
            for a in range(2):
                TS(out_bf[:, a, :], kr[:, a, :], dec_bc[a], None, ALU.mult, None, [rr, rconst] + list(reads), [rout])

        wq0 = WQ()
        for b in range(16):
            wq0.add(wsrc(w_kv, b))
        memT = Mf[:, 0:KC * 256].rearrange("p (k t) -> p k t", k=KC)
        rmemT = Reg("memT")
        xfa = Rb[:, 0:4096].bitcast(F32)
        rxfa = Reg("xfa")
        xba = Rb[:, 4096:6144]
        rxba = Reg("xba")
        for mc in range(2):
            load_xT(memd[mc * 128:(mc + 1) * 128, :], xfa, rxfa, xba, rxba, memT[:, :, mc * 128:(mc + 1) * 128], rmemT, "xl")
        P.barrier()
        chk("memT")
        for b in range(16):
            wap, wreg = wq0.get()
            if b < 8:
                for j in range(2):
                    pb, rpb = PMM.next()
                    MMg(pb[:, 0:256], [(wap[:, kc, j * 128:(j + 1) * 128], memT[:, kc, :]) for kc in range(KC)], [wreg, rmemT], rpb)
                    CP("act", memKT[:, 2 * b + j, :], pb[:, 0:256], [rpb], [rmem])
                    wq0.pump(2)
            else:
                for mc in range(2):
                    pb, rpb = PMM.next()
                    MMg(pb[:, 0:256], [(memT[:, kc, mc * 128:(mc + 1) * 128], wap[:, kc, :]) for kc in range(KC)], [wreg, rmemT], rpb)
                    CP("act", memV[:, mc, (b - 8) * 256:(b - 7) * 256], pb[:, 0:256], [rpb], [rmem])
                    wq0.pump(2)
        P.barrier()
        chk("line314")

        Wkv = big[:, 0:KC * 8 * 256].rearrange("p (k h c) -> p k h c", k=KC, h=8)
        rWkv = Reg("Wkv")
        o = KC * 8 * 256
        cspre = big[:, o:o + NPRE * 128 * 2].bitcast(F32).rearrange("p (j c f) -> p j c f", j=NPRE, c=2)
        o += NPRE * 128 * 2
        kdp = big[:, o:o + NPRE * 8 * 2].bitcast(F32).rearrange("p (j h) -> p j h", j=NPRE)
        o += NPRE * 8 * 2
        rtab = Reg("pretab")
        xfp = [big[:, o:o + 4096].bitcast(F32)] * 2
        o += 4096
        xbp = [big[:, o + i * 2048:o + (i + 1) * 2048] for i in range(2)]
        o += 4096
        xTp = [big[:, o + i * 2048:o + (i + 1) * 2048].rearrange("p (k t) -> p k t", k=KC) for i in range(2)]
        o += 4096
        assert o <= NX + NM + NR, o
        rxfp = [Reg("xfp0")] * 2
        rxbp = [Reg("xbp0"), Reg("xbp1")]
        rxTp = [Reg("xTp0"), Reg("xTp1")]
        DMA(cspre, cs_pre, [], [rtab], "c7")
        DMA(kdp, kdec_pre, [], [rtab], "c8")
        wqp = WQ()
        for h in range(8):
            wqp.add((lambda k0, k1, h=h: w_r[2 * h, :, k0:k1, 128:256]), KC, 128, Wkv[:, :, h, 0:128], rWkv)
            wqp.add((lambda k0, k1, h=h: w_r[2 * h + 1, :, k0:k1, 0:128]), KC, 128, Wkv[:, :, h, 128:256], rWkv)
        for _ in range(16):
            wqp.get(ahead=0)
        chk("preW")
        pits = [(j, hp) for j in range(NPRE) for hp in range(4)]
        pst = {}
        pax_cur = [None]

        def P1(it):
            j, hp = pits[it]
            pq = j % 2
            if hp == 0:
                load_xT(xc[j * 128:(j + 1) * 128, :], xfp[pq], rxfp[pq], xbp[pq], rxbp[pq], xTp[pq], rxTp[pq], "xlp", cast_eng="dve", evac=("dve", "dve"))
                yield
            pb, rpb = PMM.next()
            MMg(pb, [(xTp[pq][:, kc, :], Wkv[:, kc, 2 * hp:2 * hp + 2, :].rearrange("p h c -> p (h c)")) for kc in range(KC)],
                [rxTp[pq], rWkv], rpb)
            pst[it] = (pb, rpb)
            yield

        def P2(it):
            j, hp = pits[it]
            pb, rpb = pst.pop(it)
            bv = pb.rearrange("p (h c) -> p h c", h=2)
            kp, rkp = TB.next()
            kp3 = kp[:, 0:256].rearrange("p (h f) -> p h f", h=2)
            dec = [kdp[:, j, 2 * hp:2 * hp + 1], kdp[:, j, 2 * hp + 1:2 * hp + 2]]
            rope2(bv[:, :, 0:128], cspre[:, j], dec, kp3, [rpb, rtab], rkp)
            conv.pump(1)
            yield
            vb, rvb = TB.next()
            vb3 = vb[:, 0:256].rearrange("p (h f) -> p h f", h=2)
            CP("dve", vb3, bv[:, :, 128:256], [rpb], [rvb])
            conv.pump(1)
            yield
            if hp % 2 == 0:
                pax_cur[0] = PAX.next()
            pa, rpa = pax_cur[0]
            for a in range(2):
                c0 = (2 * (hp % 2) + a) * 128
                MMg(pa[:, c0:c0 + 128], [(kp3[:, a, :], vb3[:, a, :])], [rkp, rvb], rpa)
            conv.pump(1)
            yield
            if hp % 2 == 1:
                h0 = 4 * (hp // 2)
                TT(Wst[:, h0:h0 + 4, :].rearrange("p h f -> p (h f)"), Wst[:, h0:h0 + 4, :].rearrange("p h f -> p (h f)"), pa,
                   ALU.add, [rpa] + rWst[h0:h0 + 4], rWst[h0:h0 + 4])
            yield

        for step in range(len(pits) + 1):
            gens = []
            if step < len(pits):
                gens.append(P1(step))
            if step >= 1:
                gens.append(P2(step - 1))
            while gens:
                for g_ in list(gens):
                    try:
                        next(g_)
                    except StopIteration:
                        gens.remove(g_)
        conv.flush()
        conv.ring = 2
        conv.nin = conv.i
        conv.limit = None
        for h in range(8):
            ACT(Sbf[:, h, :], Wst[:, h, :], AF.Identity, [rWst[h], rconst], [rSbf[h]], scale=gamc[:, h:h + 1])
        P.barrier()
        chk("line371")

        o = 0
        A_xf = Rb[:, o:o + 4096].bitcast(F32); o += 4096
        A_xb = Rb[:, o:o + 2048]; o += 2048
        A_bias = Rb[:, o:o + 8192].bitcast(F32).rearrange("p (c h q) -> p c h q", c=2, h=16); o += 8192
        A_cs = Rb[:, o:o + MAXT * 256].bitcast(F32).rearrange("p (s c f) -> p s c f", s=MAXT, c=2); o += MAXT * 256
        A_gng = Rb[:, o:o + 2048].bitcast(F32); o += 2048
        A_KT = Rb[:, o:o + 4 * MAXT * 128].rearrange("p (h s t) -> p h s t", h=4, s=MAXT); o += 4 * MAXT * 128
        A_V = Rb[:, o:o + MAXT * 512].rearrange("p (s h a f) -> p s h a f", s=MAXT, h=4, a=2); o += MAXT * 512
        A_qs = [Rb[:, 0:2 * 896].rearrange("p (m t) -> p m t", m=2),
                Rb[:, o:o + 2 * 896].rearrange("p (m t) -> p m t", m=2)]; o += 2 * 896
        assert o <= NR, o
        LNT = Xf[:, 0:8192].bitcast(F32).rearrange("p (a d) -> p a d", a=2)
        rowb = [Mf[:, i * 2048:(i + 1) * 2048] for i in range(2)]
        HID = Mf[:, 0:11 * 896].rearrange("p (f t) -> p f t", f=11)

        def layernorm(s, rlnt):
            sm, rsm = SM.next()
            for c in range(4):
                P.op("dve", lambda e, c=c: e.bn_stats(out=sm[:, c * 6:(c + 1) * 6], in_=R[:, s, c * 512:(c + 1) * 512]),
                     reads=[regR[s]], writes=[rsm])
            P.op("dve", lambda e: e.bn_aggr(out=sm[:, 24:26], in_=sm[:, 0:24]), reads=[rsm], writes=[rsm])
            ACT(sm[:, 25:26], sm[:, 25:26], AF.Sqrt, [rsm], [rsm], bias=EPS)
            P.op("dve", lambda e: e.reciprocal(out=sm[:, 25:26], in_=sm[:, 25:26]), reads=[rsm], writes=[rsm])
            TS(sm[:, 26:27], sm[:, 24:25], -1.0, sm[:, 25:26], ALU.mult, ALU.mult, [rsm], [rsm])
            ACT(R[:, s, :], R[:, s, :], AF.Identity, [regR[s], rsm], [regR[s]], scale=sm[:, 25:26], bias=sm[:, 26:27])
            TT(R[:, s, :], R[:, s, :], LNT[:, 0, :], ALU.mult, [regR[s], rlnt], [regR[s]])
            TT(R[:, s, :], R[:, s, :], LNT[:, 1, :], ALU.add, [regR[s], rlnt], [regR[s]])

        def load_lnt(i, rlnt):
            DMA(LNT[:, 0, :], lnp_d[2 * i].partition_broadcast(128), [], [rlnt], "lnt0")
            DMA(LNT[:, 1, :], lnp_d[2 * i + 1].partition_broadcast(128), [], [rlnt], "lnt1")

        def R_to_X(slots):
            rrow = [Reg("rowb0"), Reg("rowb1")]
            for s in slots:
                pq = s % 2
                load_xT(R[:, s, :], None, None, rowb[pq], rrow[pq], X[:, :, s * 128:(s + 1) * 128], regX[s], None,
                        src_is_sbuf=True, src_reads=[regR[s]])

        def proj_resid(wq, nblk, srcT, sregs, slots, rlnt=None):
            for cb in range(nblk):
                wap, wreg = wq.get()
                for s in slots:
                    pb, rpb = PMM.next()
                    MMg(pb[:, 0:256], [(srcT[:, kc, s * 128:(s + 1) * 128], wap[:, kc, :]) for kc in range(KC)], [wreg, sregs[s]], rpb)
                    dst = R[:, s, cb * 256:(cb + 1) * 256]
                    STT(dst, dst, ALPHA, pb[:, 0:256], ALU.mult, ALU.add, [rpb, regR[s]], [regR[s]])
                    if rlnt is not None and cb == nblk - 1:
                        layernorm(s, rlnt)
                    wq.pump(1)

        for g, tl in enumerate(GROUPS):
            nt = len(tl)
            ntok = nt * 128
            slots = list(range(nt))
            own = [s for s in slots if tl[s] >= 0]
            wq = WQD()
            for h in range(8):
                wq.add("w_r", 2 * h); wq.add("w_r", 2 * h + 1)
            for b in (4, 5, 6, 0, 1, 2, 3):
                wq.add("w_s", b)
            for b in range(8):
                wq.add("w_o", b)
            for b in range(8):
                wq.add("w_q", b)
            for b in range(8):
                wq.add("w_xo", b)
            for fb in range(4):
                for fi in range(11):
                    wq.add("w_up", fb * 11 + fi)
                for cb in range(8):
                    wq.add("w_dn", fb * 8 + cb, 11)

            rAxf, rAxb, rAtab = Reg("Axf"), Reg("Axb"), Reg("Atab")
            DMA(A_bias, biasT_d, [], [rAtab], "c9")
            DMA(A_gng, gng_d.partition_broadcast(128), [], [rAtab], "c10")
            i0 = 1 + tl[0]
            DMA(A_cs[:, 0:nt], cs_main[:, i0:i0 + nt], [], [rAtab], "c11")
            if g == 0:
                load_xT(xc[(NPRE - 1) * 128:NPRE * 128, :], A_xf, rAxf, A_xb, rAxb, X[:, :, 7 * 128:8 * 128], regX[7], "xl")
            for s in slots:
                t = xc_tile(tl[s])
                load_xT(xc[t * 128:(t + 1) * 128, :], A_xf, rAxf, A_xb, rAxb, X[:, :, s * 128:(s + 1) * 128], regX[s], "xl")

            chk("A0")
            its = [(h, s) for h in range(8) for s in slots]
            hw = {}
            stt = {}
            TB2 = Pool([tb_t[:].rearrange("p a c -> p (a c)")[:, i * 256:(i + 1) * 256] for i in range(16)], "tb2")
            tff = tf_t[:].rearrange("p a c -> p (a c)")
            TFr = Pool([tff[:, i * 520:(i + 1) * 520] for i in range(2)], "tfr")
            TFsg = Pool([tff[:, 1040 + i * 128:1040 + (i + 1) * 128] for i in range(6)], "tfsg")
            TFkr = Pool([tff[:, 1808 + i * 256:1808 + (i + 1) * 256] for i in range(2)], "tfkr")
            TFon = Pool([tff[:, 2320 + i * 128:2320 + (i + 1) * 128] for i in range(3)], "tfon")

            def S1(i):
                h, s = its[i]
                if s == slots[0]:
                    hw[h] = wq.get_pair()
                wpair, wregs = hw[h]
                pb, rpb = PMM.next()
                xs = [X[:, kc, s * 128:(s + 1) * 128] for kc in range(KC)]
                MMg(pb[:, 0:512], [(xs[kc], wpair[:, kc, :]) for kc in range(KC)], wregs + [regX[s]], rpb)
                yield
                stt[i] = dict(pb=pb, rpb=rpb)

            def S2(i):
                h, s = its[i]
                d = stt[i]
                pb, rpb = d["pb"], d["rpb"]
                qk, rqk = TB2.next()
                qk3 = qk.rearrange("p (h f) -> p h f", h=2)
                dec = [qkdec[:, 0, h:h + 1], qkdec[:, 1, h:h + 1]]
                ta, ra = TFr.next()
                A = ta[:, 0:256].rearrange("p (h c f) -> p h c f", h=2, c=2)
                Bv = ta[:, 256:512].rearrange("p (h c f) -> p h c f", h=2, c=2)
                src3 = pb[:, 0:256].rearrange("p (h f) -> p h f", h=2)
                csb = A_cs[:, s].unsqueeze(1).to_broadcast([128, 2, 2, 64])
                x1 = src3[:, :, 0:64].unsqueeze(2).to_broadcast([128, 2, 2, 64])
                x2 = src3[:, :, 64:128].unsqueeze(2).to_broadcast([128, 2, 2, 64])
                TT(A, x1, csb, ALU.mult, [rpb, rAtab], [ra])
                yield
                TT(Bv, x2, csb, ALU.mult, [rpb, rAtab, ra], [ra])
                yield
                vb, rvb = TB2.next()
                CP("act", vb[:, 0:128], pb[:, 256:384], [rpb], [rvb])
                yield
                sg, rsg = TFsg.next()
                ACT(sg[:, 0:128], pb[:, 384:512], AF.Silu, [rpb], [rsg])
                yield
                tr_, rr = TFkr.next()
                kr = tr_[:, 0:256].rearrange("p (h f) -> p h f", h=2)
                TT(kr[:, :, 0:64], A[:, :, 0, :], Bv[:, :, 1, :], ALU.subtract, [ra], [rr])
                yield
                TT(kr[:, :, 64:128], A[:, :, 1, :], Bv[:, :, 0, :], ALU.add, [ra, rr], [rr])
                yield
                for a in range(2):
                    TS(qk3[:, a, :], kr[:, a, :], dec[a], None, ALU.mult, None, [rr, rconst], [rqk])
                    yield
                pt, rpt = PTR.next()
                TR(pt[:, 0:128], qk3[:, 0, :], [rqk], rpt, mark=False)
                yield
                TR(pt[:, 128:256], qk3[:, 1, :], [rqk], rpt, mark=True)
                yield
                qkT, rqkT = TB2.next()
                CP("act", qkT[:, 0:256], pt[:, 0:256], [rpt], [rqkT])
                yield
                d.update(qk3=qk3, rqk=rqk, vb=vb, rvb=rvb, sg=sg, rsg=rsg, qkT=qkT, rqkT=rqkT)

            def S3(i):
                h, s = its[i]
                d = stt[i]
                qkT, rqkT, vb, rvb, qk3, rqk = d["qkT"], d["rqkT"], d["vb"], d["rvb"], d["qk3"], d["rqk"]
                qT = qkT[:, 0:128]
                kT = qkT[:, 128:256]
                pa, rpa = PAX.next()
                MMg(pa[:, 0:128], [(kT, qT)], [rqkT], rpa)
                stm, rstm = TB2.next()
                TT(stm[:, 0:128], pa[:, 0:128], mask01[:], ALU.mult, [rpa, rconst], [rstm])
                yield
                MMg(pa[:, 128:256], [(stm[:, 0:128], vb[:, 0:128]), (qT, Sbf[:, h, :])], [rstm, rvb, rqkT, rSbf[h]], rpa)
                yield
                MMg(pa[:, 256:384], [(qk3[:, 1, :], vb[:, 0:128])], [rqk, rvb], rpa)
                STT(Wst[:, h, :], Wst[:, h, :], gamc[:, h:h + 1], pa[:, 256:384], ALU.mult, ALU.add, [rpa, rWst[h], rconst], [rWst[h]])
                yield
                ACT(Sbf[:, h, :], Wst[:, h, :], AF.Identity, [rWst[h], rconst], [rSbf[h]], scale=gamc[:, h:h + 1])
                yield
                d.update(pa=pa, rpa=rpa)

            def S4(i):
                h, s = its[i]
                d = stt[i]
                pa, rpa, sg, rsg = d["pa"], d["rpa"], d["sg"], d["rsg"]
                sm, rsm = SM.next()
                P.op("dve", lambda e, sm=sm, pa=pa: e.bn_stats(out=sm[:, 0:6], in_=pa[:, 128:256]), reads=[rpa], writes=[rsm])
                yield
                P.op("dve", lambda e, sm=sm: e.bn_aggr(out=sm[:, 6:8], in_=sm[:, 0:6]), reads=[rsm], writes=[rsm])
                yield
                ACT(sm[:, 7:8], sm[:, 7:8], AF.Sqrt, [rsm], [rsm], bias=EPS)
                yield
                P.op("dve", lambda e, sm=sm: e.reciprocal(out=sm[:, 7:8], in_=sm[:, 7:8]), reads=[rsm], writes=[rsm])
                yield
                TS(sm[:, 8:9], sm[:, 6:7], -1.0, sm[:, 7:8], ALU.mult, ALU.mult, [rsm], [rsm])
                yield
                on, ron = TFon.next()
                ACT(on[:, 0:128], pa[:, 128:256], AF.Identity, [rpa, rsm], [ron], scale=sm[:, 7:8], bias=sm[:, 8:9])
                yield
                TT(on[:, 0:128], on[:, 0:128], A_gng[:, h * 128:(h + 1) * 128], ALU.mult, [ron, rAtab], [ron])
                yield
                orb, rorb = TB2.next()
                TT(orb[:, 0:128], on[:, 0:128], sg[:, 0:128], ALU.mult, [ron, rsg], [rorb])
                yield
                pt2, rpt2 = PTR.next()
                TR(pt2[:, 0:128], orb[:, 0:128], [rorb], rpt2, mark=True)
                yield
                CP("act", M[:, h, s * 128:(s + 1) * 128], pt2[:, 0:128], [rpt2], [regM[s]])
                yield
                conv.pump(2)
                del stt[i]

            stages = (S1, S2, S3, S4)
            for step in range(len(its) + 3):
                gens = []
                for lag, fn_ in enumerate(stages):
                    i = step - lag
                    if 0 <= i < len(its):
                        gens.append(fn_(i))
                while gens:
                    for g_ in list(gens):
                        try:
                            next(g_)
                        except StopIteration:
                            gens.remove(g_)
            P.barrier()
            chk("A1")
            rKT = Reg("A_KT")
            rV = Reg("A_V")
            cgs = colgroups(ntok)
            for kb in range(2):
                wap, wreg = wq.get()
                for j in range(2):
                    kvh = 2 * kb + j
                    lw = [wap[:, kc, j * 128:(j + 1) * 128] for kc in range(KC)]
                    if g == 0:
                        pb, rpb = PMM.next()
                        MMg(pb[:, 0:128], [(lw[kc], X[:, kc, 7 * 128:8 * 128]) for kc in range(KC)], [wreg, regX[7]], rpb)
                        CP("act", prevK[:, kvh, :], pb[:, 0:128], [rpb], [rprev])
                    for (c0, n) in cgs:
                        pb, rpb = PMM.next()
                        MMg(pb[:, 0:n], [(lw[kc], X[:, kc, c0:c0 + n]) for kc in range(KC)], [wreg] + regX[0:nt], rpb)
                        CP("act", A_KT[:, kvh, c0 // 128:(c0 + n) // 128, :], pb[:, 0:n].rearrange("p (s t) -> p s t", t=128), [rpb], [rKT])
                        wq.pump(1)
            wap, wreg = wq.get()
            for s in ([7] if g == 0 else []) + slots:
                pb, rpb = PMM.next()
                MMg(pb[:, 0:256], [(X[:, kc, s * 128:(s + 1) * 128], wap[:, kc, :]) for kc in range(KC)], [wreg, regX[s]], rpb)
                src = pb[:, 0:256].rearrange("p (h f) -> p h f", h=4).unsqueeze(2).to_broadcast([128, 4, 2, 64])
                if s == 7:
                    CP("dve", prevV[:], src, [rpb], [rprev])
                else:
                    CP("dve", A_V[:, s], src, [rpb], [rV])
                wq.pump(1)
            rqs_regs = [rAxf, Reg("A_qs1")]
            for qb in range(4):
                wap, wreg = wq.get()
                qs = A_qs[qb % 2]
                for mi in range(2):
                    for (c0, n) in cgs:
                        pb, rpb = PMM.next()
                        MMg(pb[:, 0:n], [(wap[:, kc, mi * 128:(mi + 1) * 128], X[:, kc, c0:c0 + n]) for kc in range(KC)], [wreg] + regX[0:nt], rpb)
                        CP("act", qs[:, mi, c0:c0 + n], pb[:, 0:n], [rpb], [rqs_regs[qb % 2]])
                        wq.pump(1)
                rq = rqs_regs[qb % 2]
                ptsd = {}

                def stA(s):
                    pts = []
                    for c in range(2):
                        banks = [PAX.next(), PMM.next()]
                        pt_, rpt_ = TB.next()
                        pt4 = pt_.rearrange("p (m a q) -> p m a q", m=2, a=2)
                        for a in range(2):
                            pa, rpa = banks[a]
                            for mi in range(2):
                                if c == 1:
                                    kap, kr_ = A_KT[a * 64:(a + 1) * 64, qb, s, :], rKT
                                elif s == 0:
                                    kap, kr_ = prevK[a * 64:(a + 1) * 64, qb, :], rprev
                                else:
                                    kap, kr_ = A_KT[a * 64:(a + 1) * 64, qb, s - 1, :], rKT
                                MMg(pa[:, mi * 128:(mi + 1) * 128], [(kap, qs[a * 64:(a + 1) * 64, mi, s * 128:(s + 1) * 128])], [kr_, rq], rpa)
                                yield
                        for a in range(2):
                            pa, rpa = banks[a]
                            lg, rlg = TF.next()
                            bia = A_bias[:, c, 4 * qb:4 * qb + 4, :].rearrange("p (m a) q -> p m a q", a=2)[:, :, a, :]
                            STT(lg[:, 0:256].rearrange("p (m q) -> p m q", m=2), pa[:, 0:256].rearrange("p (m q) -> p m q", m=2), 0.125, bia,
                                ALU.mult, ALU.add, [rpa, rAtab], [rlg])
                            ACT(pt4[:, :, a, :], lg[:, 0:256].rearrange("p (m q) -> p m q", m=2), AF.Exp, [rlg], [rpt_])
                            yield
                        if c == 0 and tl[s] == 0:
                            TS(pt_, pt_, flag[:, 0:1], None, ALU.mult, None, [rpt_, rconst], [rpt_])
                            yield
                        pts.append((pt_, rpt_))

                    ptsd[s] = pts

                def stB(s):
                    pts = ptsd.pop(s)
                    if s == 0:
                        vprev, rvprev = prevV[:, qb].rearrange("p a f -> p (a f)"), rprev
                    else:
                        vprev, rvprev = A_V[:, s - 1, qb].rearrange("p a f -> p (a f)"), rV
                    vcur = A_V[:, s, qb].rearrange("p a f -> p (a f)")
                    pn, rpn = PTRF.aps[0], PTRF.regs[0]
                    pd, rpd = PTRF.aps[1], PTRF.regs[1]
                    for hq in range(4):
                        cs_ = slice(hq * 128, (hq + 1) * 128)
                        MMg(pn[:, cs_], [(vprev, pts[0][0][:, cs_]), (vcur, pts[1][0][:, cs_])], [rvprev, rV, pts[0][1], pts[1][1]], rpn)
                        yield
                    for hq in range(4):
                        cs_ = slice(hq * 128, (hq + 1) * 128)
                        MMg(pd[:, cs_], [(ones[:], pts[0][0][:, cs_]), (ones[:], pts[1][0][:, cs_])], [rconst, pts[0][1], pts[1][1]], rpd)
                        yield
                    rc, rrc = TF.next()
                    for hq in range(4):
                        mi, a = hq // 2, hq % 2
                        m = 2 * qb + mi
                        ps_ = slice(a * 64, (a + 1) * 64)
                        cs_ = slice(hq * 128, (hq + 1) * 128)
                        TS(rc[ps_, cs_], pd[ps_, cs_], esink[ps_, m:m + 1], None, ALU.add, None, [rpd, rconst], [rrc])
                        yield
                        P.op("dve", lambda e, ps_=ps_, cs_=cs_, rc=rc: e.reciprocal(out=rc[ps_, cs_], in_=rc[ps_, cs_]), reads=[rrc], writes=[rrc])
                        yield
                        TT(M[ps_, 8 + m, s * 128:(s + 1) * 128], pn[ps_, cs_], rc[ps_, cs_], ALU.mult, [rpn, rrc], [regM[s]])
                        yield
                    wq.pump(1)

                for k_ in range(len(slots) + 1):
                    gens = []
                    if k_ < len(slots):
                        gens.append(stA(slots[k_]))
                    if k_ > 0:
                        gens.append(stB(slots[k_ - 1]))
                    while gens:
                        for g_ in list(gens):
                            try:
                                next(g_)
                            except StopIteration:
                                gens.remove(g_)
            CP("act", prevK[:], A_KT[:, :, nt - 1, :], [rKT, rprev], [rprev])
            CP("act", prevV[:], A_V[:, nt - 1], [rV, rprev], [rprev])
            P.barrier()
            chk("line595")

            for s in slots:
                t = xc_tile(tl[s])
                DMA(R[:, s, :], xc[t * 128:(t + 1) * 128, :], [], [regR[s]], "xr%d" % s)
            rlnt = Reg("lnt")
            load_lnt(0, rlnt)
            proj_resid(wq, 8, M, regM, slots, rlnt)
            P.barrier()
            chk("line607")
            R_to_X(slots)
            for hd in range(4):
                wa, rwa = wq.get(ahead=3)
                wb, rwb = wq.get(ahead=2)
                for (c0, n) in cgs:
                    qts = []
                    for c in range(4):
                        wap, wreg = (wa, rwa) if c < 2 else (wb, rwb)
                        pb, rpb = PMM.next()
                        MMg(pb[:, 0:n], [(wap[:, kc, (c % 2) * 128:(c % 2 + 1) * 128], X[:, kc, c0:c0 + n]) for kc in range(KC)],
                            [wreg] + regX[0:nt], rpb)
                        qt, rqt = TB.next()
                        CP("act" if c % 2 == 0 else "dve", qt[:, 0:n], pb[:, 0:n], [rpb], [rqt])
                        qts.append((qt, rqt))
                    pts = []
                    for mc in range(2):
                        pa, rpa = PAX.next()
                        MMg(pa[:, 0:n], [(memKT[:, hd * 4 + c, mc * 128:(mc + 1) * 128], qts[c][0][:, 0:n]) for c in range(4)],
                            [rmem] + [q[1] for q in qts], rpa)
                        pt_, rpt_ = TB.next()
                        ACT(pt_[:, 0:n], pa[:, 0:n], AF.Exp, [rpa], [rpt_], scale=512.0 ** -0.5)
                        pts.append((pt_, rpt_))
                    pd, rpd = PAX.next()
                    MMg(pd[:, 0:n], [(ones[:], pts[mc][0][:, 0:n]) for mc in range(2)], [rconst, pts[0][1], pts[1][1]], rpd)
                    rc, rrc = TF.next()
                    P.op("dve", lambda e, rc=rc, pd=pd, n=n: e.reciprocal(out=rc[:, 0:n], in_=pd[:, 0:n]), reads=[rpd], writes=[rrc])
                    for c in range(4):
                        pb, rpb = PMM.next()
                        MMg(pb[:, 0:n], [(memV[:, mc, hd * 512 + c * 128:hd * 512 + (c + 1) * 128], pts[mc][0][:, 0:n]) for mc in range(2)],
                            [rmem, pts[0][1], pts[1][1]], rpb)
                        TT(M[:, hd * 4 + c, c0:c0 + n], pb[:, 0:n], rc[:, 0:n], ALU.mult, [rpb, rrc], regM[c0 // 128:(c0 + n + 127) // 128])
                    wq.pump(4)
            P.barrier()
            chk("line644")
            rlnt = Reg("lnt")
            load_lnt(1, rlnt)
            proj_resid(wq, 8, M, regM, slots, rlnt)
            P.barrier()
            chk("line652")
            R_to_X(slots)
            rhid = Reg("hid")
            for fb in range(4):
                for fi in range(11):
                    f = fb * 11 + fi
                    wap, wreg = wq.get()
                    for (c0, n) in cgs:
                        bu, rbu = PMM.next()
                        MMg(bu[:, 0:n], [(wap[:, kc, 0:128], X[:, kc, c0:c0 + n]) for kc in range(KC)], [wreg] + regX[0:nt], rbu)
                        bg, rbg = PMM.next()
                        MMg(bg[:, 0:n], [(wap[:, kc, 128:256], X[:, kc, c0:c0 + n]) for kc in range(KC)], [wreg] + regX[0:nt], rbg)
                        gb, rgb = TF.next()
                        CP("dve", gb[:, 0:2], gtail[:, f, :], [rgtail], [rgb])
                        CP("act", gb[:, 2:2 + n], bg[:, 0:n], [rbg, rgb], [rgb])
                        if g == 0 and c0 == 0:
                            TS(gb[:, 128:130], gb[:, 128:130], flag[:, 0:1], None, ALU.mult, None, [rgb, rconst], [rgb])
                        gc, rgc = TF.next()
                        ACT(gc[:, 0:n], bg[:, 0:n], AF.Identity, [rbg, rconst], [rgc], scale=convp[:, f, 2:3], bias=convp[:, f, 3:4])
                        STT(gc[:, 0:n], gb[:, 1:1 + n], convp[:, f, 1:2], gc[:, 0:n], ALU.mult, ALU.add, [rgb, rgc, rconst], [rgc])
                        STT(gc[:, 0:n], gb[:, 0:n], convp[:, f, 0:1], gc[:, 0:n], ALU.mult, ALU.add, [rgb, rgc, rconst], [rgc])
                        CP("dve", gtail[:, f, :], gb[:, n:n + 2], [rgb], [rgtail])
                        ACT(gc[:, 0:n], gc[:, 0:n], AF.Silu, [rgc], [rgc])
                        TT(HID[:, fi, c0:c0 + n], gc[:, 0:n], bu[:, 0:n], ALU.mult, [rgc, rbu], [rhid])
                        wq.pump(2)
                if fb == 3:
                    rlnt3 = Reg("lnt3")
                    DMA(LNT[:, 0, :], lnp_d[4].partition_broadcast(128), [], [rlnt3] + regX[0:nt], "lnt0")
                    DMA(LNT[:, 1, :], lnp_d[5].partition_broadcast(128), [], [rlnt3] + regX[0:nt], "lnt1")
                for cb in range(8):
                    wap, wreg = wq.get()
                    for s in own:
                        pb, rpb = PMM.next()
                        MMg(pb[:, 0:256], [(HID[:, fi, s * 128:(s + 1) * 128], wap[:, fi, :]) for fi in range(11)], [wreg, rhid], rpb)
                        dst = R[:, s, cb * 256:(cb + 1) * 256]
                        if fb == 0:
                            STT(dst, dst, ALPHA, pb[:, 0:256], ALU.mult, ALU.add, [rpb, regR[s]], [regR[s]])
                        else:
                            TT(dst, dst, pb[:, 0:256], ALU.add, [rpb, regR[s]], [regR[s]])
                        if fb == 3 and cb == 7:
                            layernorm(s, rlnt3)
                            DMA(y[tl[s] * 128:(tl[s] + 1) * 128, :], R[:, s, :], [regR[s]], [], "out%d" % s)
            P.barrier()
            chk("line691")
        P.op("sp", None)
        nsem = P.emit(nc, st)
        print("build: sems=%d ops=%s" % (nsem, {e: len(P.ops[e]) for e in P.ENG}), flush=True)
    return nc


def _tile_w(W, cb=256):
    K, N = W.shape
    return np.ascontiguousarray(W.reshape(K // 128, 128, N // cb, cb).transpose(2, 1, 0, 3))


def _t5_bucket(n):
    n = np.asarray(n)
    nf = np.maximum(n, 1).astype(np.float32)
    large = 16 + (np.log(nf / np.float32(16)) / np.float32(math.log(128 / 16)) * np.float32(16)).astype(np.int32)
    large = np.minimum(large, 31)
    return np.where(n < 16, n, large)


_NC_CACHE = {}


def kernel(x, mem, w_in, ret_gn_g, swa_sinks, rel_bias, w_o, ln1_g, ln1_b,
           xa_wq, xa_wkv, xa_wo, ln2_g, ln2_b,
           ffn_w_up, ffn_conv_w, ffn_conv_b, ffn_w_down, ln3_g, ln3_b):
    f32 = np.float32
    x = np.asarray(x, f32)
    mem = np.asarray(mem, f32)
    w_in = np.asarray(w_in, f32)[0]
    B, S, _ = x.shape
    q_r, k_r, v_r, g_r = w_in[:, 0:1024], w_in[:, 1024:2048], w_in[:, 2048:3072], w_in[:, 3072:4096]
    q_s, k_s, v_s = w_in[:, 4096:5120], w_in[:, 5120:5376], w_in[:, 5376:5632]
    cols = []
    for h in range(8):
        sl = slice(h * 128, (h + 1) * 128)
        cols += [q_r[:, sl], k_r[:, sl], v_r[:, sl], g_r[:, sl]]
    w_r_t = _tile_w(np.concatenate(cols, axis=1))
    kd = []
    for kh in range(4):
        kd += [k_s[:, kh * 64:(kh + 1) * 64]] * 2
    w_s_t = _tile_w(np.concatenate([q_s] + kd + [v_s], axis=1))
    w_o_t = _tile_w(np.asarray(w_o, f32)[0])
    w_q_t = _tile_w(np.asarray(xa_wq, f32)[0])
    w_kv_t = _tile_w(np.asarray(xa_wkv, f32)[0])
    w_xo_t = _tile_w(np.asarray(xa_wo, f32)[0])
    wu = np.asarray(ffn_w_up, f32)[0]
    ucols = []
    for f in range(NF):
        ucols += [wu[:, f * 128:(f + 1) * 128], wu[:, FF + f * 128:FF + (f + 1) * 128]]
    w_up_t = _tile_w(np.concatenate(ucols, axis=1))
    wd = np.asarray(ffn_w_down, f32)[0]
    w_dn_t = np.ascontiguousarray(wd.reshape(4, 11, 128, 8, 256).transpose(0, 3, 2, 1, 4))
    sinks = np.asarray(swa_sinks, f32)[0]
    p = np.arange(128)
    sinkrep = np.stack([sinks[2 * m + (p >= 64)] for m in range(8)], axis=1).astype(f32)
    rb = np.asarray(rel_bias, f32)
    jj = np.arange(128)[:, None]
    ii = np.arange(128)[None, :]
    biasT = np.full((128, 2, 16, 128), NEG, f32)
    d_prev = ii + 128 - jj
    d_cur = ii - jj
    bp = rb[_t5_bucket(np.clip(d_prev, 0, 127))]
    bc = rb[_t5_bucket(np.clip(d_cur, 0, 127))]
    vp = (d_prev < 128)
    vc = (d_cur >= 0)
    biasT[:, 0] = np.where(vp[:, None, :], bp.transpose(0, 2, 1), NEG)
    biasT[:, 1] = np.where(vc[:, None, :], bc.transpose(0, 2, 1), NEG)
    lnp = np.stack([np.asarray(a, f32)[0] for a in (ln1_g, ln1_b, ln2_g, ln2_b, ln3_g, ln3_b)])
    cw = np.asarray(ffn_conv_w, f32)[0]
    cbv = np.asarray(ffn_conv_b, f32)[0]
    convp = np.zeros((128, NF, 4), f32)
    for t in range(3):
        convp[:, :, t] = cw[t].reshape(NF, 128).T
    convp[:, :, 3] = cbv.reshape(NF, 128).T
    gng = np.asarray(ret_gn_g, f32)[0]
    lg = np.log1p(-np.exp2(-5.0 - np.arange(8, dtype=np.float64)))
    pp = np.arange(128, dtype=np.float64)
    qkdec = np.zeros((128, 2, 8), np.float64)
    qkdec[:, 0, :] = np.exp(lg[None, :] * (pp[:, None] + 1.0))
    qkdec[:, 1, :] = np.exp(-lg[None, :] * (pp[:, None] + 1.0)) * (128.0 ** -0.5)
    gamc = np.tile(np.exp(lg * 128.0)[None, :], (128, 1))
    jv = np.arange(NPRE, dtype=np.float64)
    expo = (NPRE * 128 - 1 - 128.0) - 128.0 * jv[None, :, None] - pp[:, None, None]
    kdec_pre = np.exp(lg[None, None, :] * expo) * (128.0 ** -0.5)
    mask01 = (jj <= ii).astype(f32)
    ident = np.eye(128, dtype=f32)
    inv = (1.0 / (10000.0 ** (np.arange(64, dtype=f32) / f32(64)))).astype(f32)

    shared = dict(w_r=w_r_t, w_s=w_s_t, w_o=w_o_t, w_q=w_q_t, w_kv=w_kv_t, w_xo=w_xo_t, w_up=w_up_t, w_dn=w_dn_t,
                  kdec_pre=kdec_pre.astype(f32), qkdec=qkdec.astype(f32), mask01=mask01, ident=ident, biasT=biasT,
                  sinkrep=sinkrep, gng=gng, lnp=lnp, convp=convp, gamc=gamc.astype(f32))
    in_maps = []
    per = S // 4
    for c in range(8):
        b, q = c // 4, c % 4
        start = q * per
        lo = start - (NPRE + 1) * 128
        xcc = np.zeros((nxc() * 128, D), f32)
        src_lo = max(lo, 0)
        xcc[src_lo - lo:] = x[b, src_lo:start + per]
        pos = (lo + np.arange(nxc() * 128)).astype(f32)
        ang = pos[:, None] * inv[None, :]
        cs = np.stack([np.cos(ang), np.sin(ang)], axis=1).astype(f32)
        cs = cs.reshape(nxc(), 128, 2, 64).transpose(1, 0, 2, 3)
        m = dict(shared)
        m.update(xc=xcc, memb=np.ascontiguousarray(mem[b]),
                 cs_pre=np.ascontiguousarray(cs[:, :NPRE]), cs_main=np.ascontiguousarray(cs[:, NPRE:]),
                 flag=np.full((128, 1), 0.0 if q == 0 else 1.0, f32))
        in_maps.append(m)
    if "nc" not in _NC_CACHE:
        _NC_CACHE["nc"] = build()
    res = run_bass_kernel_spmd(_NC_CACHE["nc"], in_maps, core_ids=list(range(8)))
    out = np.zeros((B, S, D), f32)
    for c in range(8):
        b, q = c // 4, c % 4
        out[b, q * per:(q + 1) * per] = res.results[c]["y"]
    return out
```

```python
import bisect
import math
from contextlib import ExitStack
import numpy as np
import concourse.bass as bass
import concourse.mybir as mybir
from concourse.bass_utils import run_bass_kernel_spmd

F32 = mybir.dt.float32
BF16 = mybir.dt.bfloat16
AF = mybir.ActivationFunctionType
ALU = mybir.AluOpType
class Reg:
    __slots__ = ("name", "w", "rc", "rd", "excl")

    def __init__(self, name, excl=False):
        self.name = name
        self.excl = excl
        self.w = None
        self.rc = {}
        self.rd = {}


class Prog:
    ENG = ["pe", "act", "dve", "pool", "sp"]

    def __init__(self):
        self.ops = {e: [] for e in self.ENG}
        self.marks = {e: [] for e in self.ENG}
        self.known = {e: {} for e in self.ENG}
        self.dcount = {}
        self.bar = {}
        self.stopped = False

    def _ctoken(self, eng, seq):
        marks = self.marks[eng]
        i = bisect.bisect_left(marks, seq)
        if i == len(marks):
            last = len(self.ops[eng]) - 1
            assert last >= seq
            marks.append(last)
            self.ops[eng][last]["inc"] = True
            i = len(marks) - 1
        return (("e", eng), i + 1)

    def barrier(self):
        if self.stopped:
            return
        toks = {}
        for e in self.ENG:
            if e == "sp" or not self.ops[e]:
                continue
            k, v = self._ctoken(e, len(self.ops[e]) - 1)
            toks[k] = v
        for k, c in self.dcount.items():
            toks[("d", k)] = 16 * c
        for e in self.ENG:
            self.bar[e] = dict(toks)

    def op(self, eng, fn, reads=(), writes=(), dkey=None, mark=None):
        if self.stopped and fn is not None:
            return None
        ex = [r for r in reads if r.excl]
        if ex:
            reads = [r for r in reads if not r.excl]
            writes = list(writes) + [r for r in ex if r not in writes]
        deps = []
        for r in reads:
            if r.w is not None:
                deps.append(r.w)
        for w in writes:
            if w.w is not None:
                deps.append(w.w)
            for e, sq in w.rc.items():
                deps.append(("c", e, sq))
            for k, v in w.rd.items():
                deps.append(("d", k, v))
        waits = {}
        for tok in deps:
            if tok[0] == "c":
                if tok[1] == eng and eng == "pe":
                    continue
                semk, val = self._ctoken(tok[1], tok[2])
            else:
                semk, val = ("d", tok[1]), tok[2]
            if self.known[eng].get(semk, 0) >= val:
                continue
            if waits.get(semk, 0) < val:
                waits[semk] = val
        if self.bar.get(eng):
            for semk, val in self.bar[eng].items():
                if semk == ("e", eng) and eng == "pe":
                    continue
                if self.known[eng].get(semk, 0) >= val:
                    continue
                if waits.get(semk, 0) < val:
                    waits[semk] = val
            self.bar[eng] = None
        for k, v in waits.items():
            self.known[eng][k] = v
        seq = len(self.ops[eng])
        rec = dict(fn=fn, waits=list(waits.items()), inc=False, dkey=dkey)
        self.ops[eng].append(rec)
        if dkey is not None:
            c = self.dcount.get(dkey, 0) + 1
            self.dcount[dkey] = c
            tok = ("d", dkey, 16 * c)
        else:
            tok = ("c", eng, seq)
            if (mark if mark is not None else (eng != "pe")):
                self.marks[eng].append(seq)
                rec["inc"] = True
        for r in reads:
            if tok[0] == "c":
                if r.rc.get(eng, -1) < seq:
                    r.rc[eng] = seq
            else:
                r.rd[dkey] = tok[2]
        for w in writes:
            w.w = tok
            w.rc = {}
            w.rd = {}
        return tok

    def emit(self, nc, stack):
        keys = set()
        for e in self.ENG:
            for rec in self.ops[e]:
                for k, _ in rec["waits"]:
                    keys.add(k)
                if rec["dkey"] is not None:
                    keys.add(("d", rec["dkey"]))
                elif rec["inc"]:
                    keys.add(("e", e))
        sems = {}
        for k in sorted(keys, key=str):
            sems[k] = stack.enter_context(nc.semaphore("s_" + "_".join(str(x) for x in k)))
        block = stack.enter_context(nc.Block())

        def run(engname):
            def f(eng):
                for rec in self.ops[engname]:
                    for k, v in rec["waits"]:
                        eng.wait_ge(sems[k], v)
                    if rec["fn"] is None:
                        continue
                    ins = rec["fn"](eng)
                    if rec["dkey"] is not None:
                        ins.then_inc(sems[("d", rec["dkey"])], 16)
                    elif rec["inc"]:
                        ins.then_inc(sems[("e", engname)], 1)
            return f

        block.tensor(run("pe"))
        block.scalar(run("act"))
        block.vector(run("dve"))
        block.gpsimd(run("pool"))
        block.sync(run("sp"))
        return len(sems)

D = 2048
KC = 16
NOWN = 16
NPRE = 48
GROUPS = [[-1, 0, 1, 2, 3, 4], [5, 6, 7, 8, 9, 10], [11, 12, 13, 14, 15]]
MAXT = 7


def set_cfg(nown, groups):
    global NOWN, NPRE, GROUPS
    NOWN, NPRE, GROUPS = nown, 3 * nown, groups


def nxc():
    return NPRE + 1 + NOWN
ALPHA = 2.0 ** 0.25
EPS = 1e-5
FF = 5632
NF = 44
NEG = -30000.0


def xc_tile(gt):
    return NPRE + 1 + gt


def colgroups(ntok, mx=512):
    out = []
    o = 0
    while o < ntok:
        n = min(mx, ntok - o)
        out.append((o, n))
        o += n
    return out


class Pool:
    def __init__(self, aps, name, excl=False):
        self.aps = aps
        self.regs = [Reg("%s%d" % (name, i), excl) for i in range(len(aps))]
        self.i = 0

    def next(self):
        j = self.i % len(self.aps)
        self.i += 1
        return self.aps[j], self.regs[j]


def build():
    import os
    nc = bass.Bass("TRN2", target_bir_lowering=False)
    P = Prog()
    kstop = int(os.environ.get("KSTOP", "-1"))
    cnt = [0]

    kstopn = os.environ.get("KSTOPN", "")

    def chk(name):
        cnt[0] += 1
        if cnt[0] == kstop or (kstopn and name == kstopn and not P.stopped):
            print("STOP at checkpoint", cnt[0], name, flush=True)
            P.stopped = True

    def din(name, shape, dt=F32):
        return nc.dram_tensor(name, list(shape), dt, kind="ExternalInput").ap()

    xc = din("xc", [nxc() * 128, D])
    memd = din("memb", [256, D])
    w_r = din("w_r", [16, 128, KC, 256])
    w_s = din("w_s", [7, 128, KC, 256])
    w_o = din("w_o", [8, 128, KC, 256])
    w_q = din("w_q", [8, 128, KC, 256])
    w_kv = din("w_kv", [16, 128, KC, 256])
    w_xo = din("w_xo", [8, 128, KC, 256])
    w_up = din("w_up", [NF, 128, KC, 256])
    w_dn = din("w_dn", [4, 8, 128, 11, 256])
    cs_pre = din("cs_pre", [128, NPRE, 2, 64])
    cs_main = din("cs_main", [128, 1 + NOWN, 2, 64])
    kdec_pre = din("kdec_pre", [128, NPRE, 8])
    qkdec_d = din("qkdec", [128, 2, 8])
    mask_d = din("mask01", [128, 128])
    ident_d = din("ident", [128, 128])
    biasT_d = din("biasT", [128, 2, 16, 128])
    sink_d = din("sinkrep", [128, 8])
    flag_d = din("flag", [128, 1])
    gng_d = din("gng", [1024])
    lnp_d = din("lnp", [6, D])
    convp_d = din("convp", [128, NF, 4])
    gamc_d = din("gamc", [128, 8])
    y = nc.dram_tensor("y", [NOWN * 128, D], F32, kind="ExternalOutput").ap()

    st = ExitStack()
    with st:
        def sb(name, shape, dt):
            return st.enter_context(nc.sbuf_tensor(name, list(shape), dt))

        NX = KC * 8 * 128
        NM = KC * MAXT * 128
        NR = MAXT * D * 2
        big = sb("big", [128, NX + NM + NR], BF16)
        Xf = big[:, 0:NX]
        X = Xf.rearrange("p (k t) -> p k t", k=KC)
        Mf = big[:, NX:NX + NM]
        M = Mf.rearrange("p (k t) -> p k t", k=KC)
        Rb = big[:, NX + NM:NX + NM + NR]
        R = Rb.bitcast(F32).rearrange("p (s d) -> p s d", s=MAXT)
        regX = [Reg("X%d" % i) for i in range(8)]
        regM = [Reg("M%d" % i) for i in range(MAXT)]
        regR = [Reg("R%d" % i) for i in range(MAXT)]

        NWS = 4
        Wt = sb("wslots", [128, NWS, KC * 256], BF16)
        Wsl = Pool([Wt[:, i, :].rearrange("p (k c) -> p k c", k=KC) for i in range(NWS)], "W")
        stg_t = sb("stg", [128, 2, 4 * 256], F32)
        STG = Pool([stg_t[:, i, :].rearrange("p (k c) -> p k c", k=4) for i in range(2)], "STG")
        tf_t = sb("tf", [128, 6, 520], F32)
        TF = Pool([tf_t[:, i, :] for i in range(6)], "tf")
        tb_t = sb("tb", [128, 8, 512], BF16)
        TB = Pool([tb_t[:, i, :] for i in range(8)], "tb")
        sm_t = sb("sm", [128, 6, 32], F32)
        SM = Pool([sm_t[:, i, :] for i in range(6)], "sm")

        ident = sb("identb", [128, 128], BF16)
        ones = sb("onesb", [128, 128], BF16)
        mask01 = sb("mask01s", [128, 128], F32)
        Wst = sb("wstate", [128, 8, 128], F32)
        Sbf = sb("sbf", [128, 8, 128], BF16)
        prevK = sb("prevK", [128, 4, 128], BF16)
        prevV = sb("prevV", [128, 4, 2, 64], BF16)
        esink = sb("esink", [128, 8], F32)
        qkdec = sb("qkdecs", [128, 2, 8], F32)
        gamc = sb("gamcs", [128, 8], F32)
        flag = sb("flags", [128, 1], F32)
        convp = sb("convps", [128, NF, 4], F32)
        gtail = sb("gtail", [128, NF, 2], F32)
        memKT = sb("memKT", [128, KC, 256], BF16)
        memV = sb("memV", [128, 2, D], BF16)
        rconst = Reg("const")
        rWst = [Reg("Wst%d" % h) for h in range(8)]
        rSbf = [Reg("Sbf%d" % h) for h in range(8)]
        rprev = Reg("prevKV")
        rgtail = Reg("gtail")
        rmem = Reg("memKV")

        pmm_t = [st.enter_context(nc.psum_tensor("pmm%d" % i, [128, 512], F32)) for i in range(4)]
        PMM = Pool([t[:] for t in pmm_t], "pmm", excl=True)
        ptr_t = [st.enter_context(nc.psum_tensor("ptr%d" % i, [128, 1024], BF16)) for i in range(2)]
        PTR = Pool([t[:] for t in ptr_t], "ptr", excl=True)
        pax_t = [st.enter_context(nc.psum_tensor("pax%d" % i, [128, 512], F32)) for i in range(2)]
        PAX = Pool([t[:] for t in pax_t], "pax", excl=True)
        PTRF = Pool([t[:].bitcast(F32) for t in ptr_t], "ptrf", excl=True)
        PTRF.regs = PTR.regs

        dctr = [0]

        def DMA(out, in_, reads, writes, key):
            P.op("sp", lambda e: e.dma_start(out=out, in_=in_), reads=reads, writes=writes, dkey=key)

        def MMg(out, pairs, reads, wreg):
            n = len(pairs)
            for i, (l, r) in enumerate(pairs):
                P.op("pe", lambda e, l=l, r=r, i=i: e.matmul(out, lhsT=l, rhs=r, start=(i == 0), stop=(i == n - 1)),
                     reads=reads, writes=[wreg], mark=(i == n - 1))

        def TR(out, in_, reads, wreg, mark):
            P.op("pe", lambda e: e.transpose(out=out, in_=in_, identity=ident[:]), reads=reads + [rconst], writes=[wreg], mark=mark)

        def ACT(out, in_, func, reads, writes, scale=1.0, bias=0.0):
            P.op("act", lambda e: e.activation(out=out, in_=in_, func=func, scale=scale, bias=bias), reads=reads, writes=writes)

        def CP(eng, out, in_, reads, writes):
            if eng == "act":
                ACT(out, in_, AF.Copy, reads, writes)
            else:
                P.op(eng, lambda e: e.tensor_copy(out=out, in_=in_), reads=reads, writes=writes)

        def TT(out, a, b, op, reads, writes, eng="dve"):
            P.op(eng, lambda e: e.tensor_tensor(out=out, in0=a, in1=b, op=op), reads=reads, writes=writes)

        def TS(out, a, s1, s2, op0, op1, reads, writes):
            if op1 is None:
                P.op("dve", lambda e: e.tensor_scalar(out=out, in0=a, scalar1=s1, scalar2=None, op0=op0), reads=reads, writes=writes)
            else:
                P.op("dve", lambda e: e.tensor_scalar(out=out, in0=a, scalar1=s1, scalar2=s2, op0=op0, op1=op1), reads=reads, writes=writes)

        def STT(out, a, s, b, op0, op1, reads, writes):
            P.op("dve", lambda e: e.scalar_tensor_tensor(out=out, in0=a, scalar=s, in1=b, op0=op0, op1=op1), reads=reads, writes=writes)

        castctr = [0]

        def stage_dma(slot_i, src, nkc, C):
            sap, sreg = STG.aps[slot_i], STG.regs[slot_i]
            DMA(sap[:, 0:nkc, 0:C], src, [], [sreg], "stg%d" % slot_i)

        def stage_cast(slot_i, dst, nkc, C, dreg):
            sap, sreg = STG.aps[slot_i], STG.regs[slot_i]
            castctr[0] += 1
            if castctr[0] % 2 == 0:
                ACT(dst, sap[:, 0:nkc, 0:C], AF.Copy, [sreg], [dreg])
            else:
                P.op("dve", lambda e: e.tensor_copy(out=dst, in_=sap[:, 0:nkc, 0:C]), reads=[sreg], writes=[dreg])

        class WQ:
            def __init__(self):
                self.plan = []
                self.slots = []
                self.pos = 0
                self.chunks = []
                self.nd = 0
                self.ncast = 0
                self.blk_end = []

            def add(self, src_fn, nk=KC, C=256, dst=None, dreg=None):
                self.plan.append((src_fn, nk, C, dst, dreg))

            def _schedule(self):
                i = len(self.slots)
                fn, nk, C, dst, dreg = self.plan[i]
                if dst is None:
                    dst, dreg = Wsl.next()
                self.slots.append((dst, dreg))
                k0 = 0
                while k0 < nk:
                    k1 = min(nk, k0 + 4)
                    self.chunks.append((i, dst[:, k0:k1, 0:C], fn(k0, k1), k1 - k0, C, dreg))
                    k0 = k1
                self.blk_end.append(len(self.chunks))

            def _dma(self):
                c = self.chunks[self.nd]
                stage_dma(self.nd % 2, c[2], c[3], c[4])
                self.nd += 1

            def _pump1(self):
                self.pump(1)

            def pump(self, n=1):
                for _ in range(n):
                    if self.ncast >= len(self.chunks):
                        return
                    while self.nd <= self.ncast:
                        self._dma()
                    c = self.chunks[self.ncast]
                    stage_cast(self.ncast % 2, c[1], c[3], c[4], c[5])
                    self.ncast += 1
                    if self.nd < len(self.chunks) and self.nd < self.ncast + 1:
                        self._dma()

            def get(self, ahead=2):
                while len(self.slots) <= min(self.pos + ahead, len(self.plan) - 1):
                    self._schedule()
                while self.ncast < self.blk_end[self.pos]:
                    self._pump1()
                r = self.slots[self.pos]
                self.pos += 1
                return r

        bst_t = sb("bst", [128, 2, 4 * 256], BF16)
        Wflat = Wt[:].rearrange("p s c -> p (s c)")
        CIN = Pool([stg_t[:, i, :].rearrange("p (k c) -> p k c", k=4) for i in range(2)] +
                   [Wflat[:, i * 2048:(i + 1) * 2048].bitcast(F32).rearrange("p (k c) -> p k c", k=4) for i in range(4)], "CIN")
        CIN.regs[0], CIN.regs[1] = STG.regs[0], STG.regs[1]
        COUT = Pool([bst_t[:, i, :].rearrange("p (k c) -> p k c", k=4) for i in range(2)] +
                    [Wflat[:, 8192 + i * 1024:8192 + (i + 1) * 1024].rearrange("p (k c) -> p k c", k=4) for i in range(4)], "COUT")
        RING = 6

        class Conv:
            def __init__(self):
                self.chunks = []
                self.i = 0
                self.nin = 0
                self.ring = RING
                self.limit = None
                self.end_of = {}

            def add_block(self, src_fn, dst_fn, nk, reg):
                k0 = 0
                while k0 < nk:
                    k1 = min(nk, k0 + 4)
                    self.chunks.append((src_fn(k0, k1), dst_fn(k0, k1), k1 - k0, reg))
                    k0 = k1
                self.end_of[id(reg)] = len(self.chunks)

            def ensure(self, reg):
                while self.i < self.end_of[id(reg)]:
                    self.pump(1)

            def _din(self, k):
                src, dst, nkc, reg = self.chunks[k]
                si = k % self.ring
                DMA(CIN.aps[si][:, 0:nkc, :], src, [], [CIN.regs[si]], "stg%d" % si if si < 2 else "cin%d" % si)
                self.nin = k + 1

            def pump(self, n=1):
                for _ in range(n):
                    lim = len(self.chunks) if self.limit is None else self.limit
                    if self.i >= lim:
                        return
                    i = self.i
                    while self.nin < min(i + self.ring, lim):
                        self._din(self.nin)
                    src, dst, nkc, reg = self.chunks[i]
                    si = i % self.ring
                    castctr[0] += 1
                    cin, cout = CIN.aps[si][:, 0:nkc, :], COUT.aps[si][:, 0:nkc, :]
                    ACT(cout, cin, AF.Copy, [CIN.regs[si]], [COUT.regs[si]])
                    DMA(dst, cout, [COUT.regs[si]], [reg], "cout%d" % si)
                    self.i += 1

            def flush(self):
                lim = len(self.chunks) if self.limit is None else self.limit
                while self.i < lim:
                    self.pump(1)

        conv = Conv()
        scr = {}
        sregs = {}

        def mkscr(name, wt, nblk, nk=KC):
            scr[name] = nc.dram_tensor("scr_" + name, [nblk, 128, nk, 256], BF16).ap()
            sregs[name] = [Reg("scr_%s%d" % (name, b)) for b in range(nblk)]

        mkscr("w_r", w_r, 16); mkscr("w_s", w_s, 7); mkscr("w_o", w_o, 8); mkscr("w_q", w_q, 8); mkscr("w_xo", w_xo, 8)
        mkscr("w_up", w_up, NF); mkscr("w_dn", w_dn, 32, 11)

        def conv_add(name, wt, b):
            conv.add_block(lambda k0, k1: wt[b, :, k0:k1, :], lambda k0, k1: scr[name][b, :, k0:k1, :], KC, sregs[name][b])

        for b in range(16):
            conv_add("w_r", w_r, b)
        for b in (4, 5, 6, 0, 1, 2, 3):
            conv_add("w_s", w_s, b)
        for b in range(8):
            conv_add("w_o", w_o, b)
        for b in range(8):
            conv_add("w_q", w_q, b)
        for b in range(8):
            conv_add("w_xo", w_xo, b)
        conv.limit = len(conv.chunks)
        for fb in range(4):
            for fi in range(11):
                conv_add("w_up", w_up, fb * 11 + fi)
            for cb in range(8):
                conv.add_block((lambda k0, k1, fb=fb, cb=cb: w_dn[fb, cb, :, k0:k1, :]),
                               (lambda k0, k1, fb=fb, cb=cb: scr["w_dn"][fb * 8 + cb, :, k0:k1, :]), 11, sregs["w_dn"][fb * 8 + cb])

        class WQD:
            def __init__(self):
                self.plan = []
                self.slots = []
                self.pos = 0

            def add(self, name, b, nk=KC):
                self.plan.append((name, b, nk))

            def pump(self, n=1):
                conv.pump(n)

            def get_pair(self):
                assert len(self.slots) == self.pos or len(self.slots) == self.pos + 2, (len(self.slots), self.pos)
                def sched():
                    if Wsl.i % 2 == 1:
                        Wsl.i += 1
                    j = Wsl.i % NWS
                    pair = Wt[:, j:j + 2, :].rearrange("p s c -> p (s c)").rearrange("p (k c) -> p k c", k=KC)
                    regs = [Wsl.regs[j], Wsl.regs[j + 1]]
                    Wsl.i += 2
                    for t in range(2):
                        name, b, nk = self.plan[len(self.slots)]
                        conv.ensure(sregs[name][b])
                        DMA(pair[:, :, t * 256:(t + 1) * 256], scr[name][b], [sregs[name][b]], regs, "wsl%d" % (j + t))
                        self.slots.append((pair, regs))
                if len(self.slots) == self.pos:
                    sched()
                r = self.slots[self.pos]
                self.pos += 2
                if len(self.slots) == self.pos and self.pos + 1 < len(self.plan) and self.plan[self.pos][0] == "w_r":
                    sched()
                return r

            def get(self, ahead=3):
                while len(self.slots) <= min(self.pos + ahead, len(self.plan) - 1):
                    name, b, nk = self.plan[len(self.slots)]
                    conv.ensure(sregs[name][b])
                    j = Wsl.i % NWS
                    dst, dreg = Wsl.next()
                    DMA(dst[:, 0:nk, :], scr[name][b], [sregs[name][b]], [dreg], "wsl%d" % j)
                    self.slots.append((dst, dreg))
                r = self.slots[self.pos]
                self.pos += 1
                return r

        def wsrc(wt, blk):
            return lambda k0, k1: wt[blk, :, k0:k1, :]

        t_, r_ = TF.next()
        DMA(t_[:, 0:128], ident_d, [], [r_], "c0")
        CP("act", ident[:], t_[:, 0:128], [r_], [rconst])
        DMA(mask01[:], mask_d, [], [rconst], "c1")
        DMA(qkdec[:], qkdec_d, [], [rconst], "c2")
        DMA(gamc[:], gamc_d, [], [rconst], "c3")
        DMA(flag[:], flag_d, [], [rconst], "c4")
        DMA(convp[:], convp_d, [], [rconst], "c5")
        DMA(esink[:], sink_d, [], [rconst], "c6")
        ACT(esink[:], esink[:], AF.Exp, [rconst], [rconst])
        P.op("dve", lambda e: e.memset(ones[:], 1.0), writes=[rconst])
        P.op("dve", lambda e: e.memset(gtail[:], 0.0), writes=[rgtail])
        for h in range(8):
            P.op("dve", lambda e, h=h: e.memset(Wst[:, h, :], 0.0), writes=[rWst[h]])

        chk("const")
        def load_xT(src_rows, xf, rxf, xb, rxb, dst3, dreg, key, src_is_sbuf=False, src_reads=(), cast_eng="act", evac=("dve", "act")):
            if not src_is_sbuf:
                DMA(xf, src_rows, [], [rxf], key)
                CP(cast_eng, xb, xf, [rxf], [rxb])
            else:
                CP("act", xb, src_rows, list(src_reads), [rxb])
            for half in range(2):
                pt, rpt = PTR.next()
                for k in range(8):
                    kc = half * 8 + k
                    TR(pt[:, k * 128:(k + 1) * 128], xb[:, kc * 128:(kc + 1) * 128], [rxb], rpt, mark=(k == 7))
                CP(evac[half], dst3[:, half * 8:half * 8 + 8, :],
                   pt.rearrange("p (k t) -> p k t", k=8), [rpt], [dreg])

        def rope2(src3, cs, dec_bc, out_bf, reads, rout):
            ta, ra = TF.next()
            A = ta[:, 0:256].rearrange("p (h c f) -> p h c f", h=2, c=2)
            Bv = ta[:, 256:512].rearrange("p (h c f) -> p h c f", h=2, c=2)
            csb = cs.unsqueeze(1).to_broadcast([128, 2, 2, 64])
            x1 = src3[:, :, 0:64].unsqueeze(2).to_broadcast([128, 2, 2, 64])
            x2 = src3[:, :, 64:128].unsqueeze(2).to_broadcast([128, 2, 2, 64])
            TT(A, x1, csb, ALU.mult, reads, [ra])
            TT(Bv, x2, csb, ALU.mult, reads + [ra], [ra])
            tr_, rr = TF.next()
            kr = tr_[:, 0:256].rearrange("p (h f) -> p h f", h=2)
            TT(kr[:, :, 0:64], A[:, :, 0, :], Bv[:, :, 1, :], ALU.subtract, [ra], [rr])
            TT(kr[:, :, 64:128], A[:, :, 1, :], Bv[:, :, 0, :], ALU.add, [ra, rr], [rr])
            for a in range(2):
                TS(out_bf[:, a, :], kr[:, a, :], dec_bc[a], None, ALU.mult, None, [rr, rconst] + list(reads), [rout])

        wq0 = WQ()
        for b in range(16):
            wq0.add(wsrc(w_kv, b))
        memT = Mf[:, 0:KC * 256].rearrange("p (k t) -> p k t", k=KC)
        rmemT = Reg("memT")
        xfa = Rb[:, 0:4096].bitcast(F32)
        rxfa = Reg("xfa")
        xba = Rb[:, 4096:6144]
        rxba = Reg("xba")
        for mc in range(2):
            load_xT(memd[mc * 128:(mc + 1) * 128, :], xfa, rxfa, xba, rxba, memT[:, :, mc * 128:(mc + 1) * 128], rmemT, "xl")
        P.barrier()
        chk("memT")
        for b in range(16):
            wap, wreg = wq0.get()
            if b < 8:
                for j in range(2):
                    pb, rpb = PMM.next()
                    MMg(pb[:, 0:256], [(wap[:, kc, j * 128:(j + 1) * 128], memT[:, kc, :]) for kc in range(KC)], [wreg, rmemT], rpb)
                    CP("act", memKT[:, 2 * b + j, :], pb[:, 0:256], [rpb], [rmem])
                    wq0.pump(2)
            else:
                for mc in range(2):
                    pb, rpb = PMM.next()
                    MMg(pb[:, 0:256], [(memT[:, kc, mc * 128:(mc + 1) * 128], wap[:, kc, :]) for kc in range(KC)], [wreg, rmemT], rpb)
                    CP("act", memV[:, mc, (b - 8) * 256:(b - 7) * 256], pb[:, 0:256], [rpb], [rmem])
                    wq0.pump(2)
        P.barrier()
        chk("line314")

        Wkv = big[:, 0:KC * 8 * 256].rearrange("p (k h c) -> p k h c", k=KC, h=8)
        rWkv = Reg("Wkv")
        o = KC * 8 * 256
        cspre = big[:, o:o + NPRE * 128 * 2].bitcast(F32).rearrange("p (j c f) -> p j c f", j=NPRE, c=2)
        o += NPRE * 128 * 2
        kdp = big[:, o:o + NPRE * 8 * 2].bitcast(F32).rearrange("p (j h) -> p j h", j=NPRE)
        o += NPRE * 8 * 2
        rtab = Reg("pretab")
        xfp = [big[:, o:o + 4096].bitcast(F32)] * 2
        o += 4096
        xbp = [big[:, o + i * 2048:o + (i + 1) * 2048] for i in range(2)]
        o += 4096
        xTp = [big[:, o + i * 2048:o + (i + 1) * 2048].rearrange("p (k t) -> p k t", k=KC) for i in range(2)]
        o += 4096
        assert o <= NX + NM + NR, o
        rxfp = [Reg("xfp0")] * 2
        rxbp = [Reg("xbp0"), Reg("xbp1")]
        rxTp = [Reg("xTp0"), Reg("xTp1")]
        DMA(cspre, cs_pre, [], [rtab], "c7")
        DMA(kdp, kdec_pre, [], [rtab], "c8")
        wqp = WQ()
        for h in range(8):
            wqp.add((lambda k0, k1, h=h: w_r[2 * h, :, k0:k1, 128:256]), KC, 128, Wkv[:, :, h, 0:128], rWkv)
            wqp.add((lambda k0, k1, h=h: w_r[2 * h + 1, :, k0:k1, 0:128]), KC, 128, Wkv[:, :, h, 128:256], rWkv)
        for _ in range(16):
            wqp.get(ahead=0)
        chk("preW")
        pits = [(j, hp) for j in range(NPRE) for hp in range(4)]
        pst = {}
        pax_cur = [None]

        def P1(it):
            j, hp = pits[it]
            pq = j % 2
            if hp == 0:
                load_xT(xc[j * 128:(j + 1) * 128, :], xfp[pq], rxfp[pq], xbp[pq], rxbp[pq], xTp[pq], rxTp[pq], "xlp", cast_eng="dve", evac=("dve", "dve"))
                yield
            pb, rpb = PMM.next()
            MMg(pb, [(xTp[pq][:, kc, :], Wkv[:, kc, 2 * hp:2 * hp + 2, :].rearrange("p h c -> p (h c)")) for kc in range(KC)],
                [rxTp[pq], rWkv], rpb)
            pst[it] = (pb, rpb)
            yield

        def P2(it):
            j, hp = pits[it]
            pb, rpb = pst.pop(it)
            bv = pb.rearrange("p (h c) -> p h c", h=2)
            kp, rkp = TB.next()
            kp3 = kp[:, 0:256].rearrange("p (h f) -> p h f", h=2)
            dec = [kdp[:, j, 2 * hp:2 * hp + 1], kdp[:, j, 2 * hp + 1:2 * hp + 2]]
            rope2(bv[:, :, 0:128], cspre[:, j], dec, kp3, [rpb, rtab], rkp)
            conv.pump(1)
            yield
            vb, rvb = TB.next()
            vb3 = vb[:, 0:256].rearrange("p (h f) -> p h f", h=2)
            CP("dve", vb3, bv[:, :, 128:256], [rpb], [rvb])
            conv.pump(1)
            yield
            if hp % 2 == 0:
                pax_cur[0] = PAX.next()
            pa, rpa = pax_cur[0]
            for a in range(2):
                c0 = (2 * (hp % 2) + a) * 128
                MMg(pa[:, c0:c0 + 128], [(kp3[:, a, :], vb3[:, a, :])], [rkp, rvb], rpa)
            conv.pump(1)
            yield
            if hp % 2 == 1:
                h0 = 4 * (hp // 2)
                TT(Wst[:, h0:h0 + 4, :].rearrange("p h f -> p (h f)"), Wst[:, h0:h0 + 4, :].rearrange("p h f -> p (h f)"), pa,
                   ALU.add, [rpa] + rWst[h0:h0 + 4], rWst[h0:h0 + 4])
            yield

        for step in range(len(pits) + 1):
            gens = []
            if step < len(pits):
                gens.append(P1(step))
            if step >= 1:
                gens.append(P2(step - 1))
            while gens:
                for g_ in list(gens):
                    try:
                        next(g_)
                    except StopIteration:
                        gens.remove(g_)
        conv.flush()
        conv.ring = 2
        conv.nin = conv.i
        conv.limit = None
        for h in range(8):
            ACT(Sbf[:, h, :], Wst[:, h, :], AF.Identity, [rWst[h], rconst], [rSbf[h]], scale=gamc[:, h:h + 1])
        P.barrier()
        chk("line371")

        o = 0
        A_xf = Rb[:, o:o + 4096].bitcast(F32); o += 4096
        A_xb = Rb[:, o:o + 2048]; o += 2048
        A_bias = Rb[:, o:o + 8192].bitcast(F32).rearrange("p (c h q) -> p c h q", c=2, h=16); o += 8192
        A_cs = Rb[:, o:o + MAXT * 256].bitcast(F32).rearrange("p (s c f) -> p s c f", s=MAXT, c=2); o += MAXT * 256
        A_gng = Rb[:, o:o + 2048].bitcast(F32); o += 2048
        A_KT = Rb[:, o:o + 4 * MAXT * 128].rearrange("p (h s t) -> p h s t", h=4, s=MAXT); o += 4 * MAXT * 128
        A_V = Rb[:, o:o + MAXT * 512].rearrange("p (s h a f) -> p s h a f", s=MAXT, h=4, a=2); o += MAXT * 512
        A_qs = [Rb[:, 0:2 * 896].rearrange("p (m t) -> p m t", m=2),
                Rb[:, o:o + 2 * 896].rearrange("p (m t) -> p m t", m=2)]; o += 2 * 896
        assert o <= NR, o
        LNT = Xf[:, 0:8192].bitcast(F32).rearrange("p (a d) -> p a d", a=2)
        rowb = [Mf[:, i * 2048:(i + 1) * 2048] for i in range(2)]
        HID = Mf[:, 0:11 * 896].rearrange("p (f t) -> p f t", f=11)

        def layernorm(s, rlnt):
            sm, rsm = SM.next()
            for c in range(4):
                P.op("dve", lambda e, c=c: e.bn_stats(out=sm[:, c * 6:(c + 1) * 6], in_=R[:, s, c * 512:(c + 1) * 512]),
                     reads=[regR[s]], writes=[rsm])
            P.op("dve", lambda e: e.bn_aggr(out=sm[:, 24:26], in_=sm[:, 0:24]), reads=[rsm], writes=[rsm])
            ACT(sm[:, 25:26], sm[:, 25:26], AF.Sqrt, [rsm], [rsm], bias=EPS)
            P.op("dve", lambda e: e.reciprocal(out=sm[:, 25:26], in_=sm[:, 25:26]), reads=[rsm], writes=[rsm])
            TS(sm[:, 26:27], sm[:, 24:25], -1.0, sm[:, 25:26], ALU.mult, ALU.mult, [rsm], [rsm])
            ACT(R[:, s, :], R[:, s, :], AF.Identity, [regR[s], rsm], [regR[s]], scale=sm[:, 25:26], bias=sm[:, 26:27])
            TT(R[:, s, :], R[:, s, :], LNT[:, 0, :], ALU.mult, [regR[s], rlnt], [regR[s]])
            TT(R[:, s, :], R[:, s, :], LNT[:, 1, :], ALU.add, [regR[s], rlnt], [regR[s]])

        def load_lnt(i, rlnt):
            DMA(LNT[:, 0, :], lnp_d[2 * i].partition_broadcast(128), [], [rlnt], "lnt0")
            DMA(LNT[:, 1, :], lnp_d[2 * i + 1].partition_broadcast(128), [], [rlnt], "lnt1")

        def R_to_X(slots):
            rrow = [Reg("rowb0"), Reg("rowb1")]
            for s in slots:
                pq = s % 2
                load_xT(R[:, s, :], None, None, rowb[pq], rrow[pq], X[:, :, s * 128:(s + 1) * 128], regX[s], None,
                        src_is_sbuf=True, src_reads=[regR[s]])

        def proj_resid(wq, nblk, srcT, sregs, slots, rlnt=None):
            for cb in range(nblk):
                wap, wreg = wq.get()
                for s in slots:
                    pb, rpb = PMM.next()
                    MMg(pb[:, 0:256], [(srcT[:, kc, s * 128:(s + 1) * 128], wap[:, kc, :]) for kc in range(KC)], [wreg, sregs[s]], rpb)
                    dst = R[:, s, cb * 256:(cb + 1) * 256]
                    STT(dst, dst, ALPHA, pb[:, 0:256], ALU.mult, ALU.add, [rpb, regR[s]], [regR[s]])
                    if rlnt is not None and cb == nblk - 1:
                        layernorm(s, rlnt)
                    wq.pump(1)

        for g, tl in enumerate(GROUPS):
            nt = len(tl)
            ntok = nt * 128
            slots = list(range(nt))
            own = [s for s in slots if tl[s] >= 0]
            wq = WQD()
            for h in range(8):
                wq.add("w_r", 2 * h); wq.add("w_r", 2 * h + 1)
            for b in (4, 5, 6, 0, 1, 2, 3):
                wq.add("w_s", b)
            for b in range(8):
                wq.add("w_o", b)
            for b in range(8):
                wq.add("w_q", b)
            for b in range(8):
                wq.add("w_xo", b)
            for fb in range(4):
                for fi in range(11):
                    wq.add("w_up", fb * 11 + fi)
                for cb in range(8):
                    wq.add("w_dn", fb * 8 + cb, 11)

            rAxf, rAxb, rAtab = Reg("Axf"), Reg("Axb"), Reg("Atab")
            DMA(A_bias, biasT_d, [], [rAtab], "c9")
            DMA(A_gng, gng_d.partition_broadcast(128), [], [rAtab], "c10")
            i0 = 1 + tl[0]
            DMA(A_cs[:, 0:nt], cs_main[:, i0:i0 + nt], [], [rAtab], "c11")
            if g == 0:
                load_xT(xc[(NPRE - 1) * 128:NPRE * 128, :], A_xf, rAxf, A_xb, rAxb, X[:, :, 7 * 128:8 * 128], regX[7], "xl")
            for s in slots:
                t = xc_tile(tl[s])
                load_xT(xc[t * 128:(t + 1) * 128, :], A_xf, rAxf, A_xb, rAxb, X[:, :, s * 128:(s + 1) * 128], regX[s], "xl")

            chk("A0")
            its = [(h, s) for h in range(8) for s in slots]
            hw = {}
            stt = {}
            TB2 = Pool([tb_t[:].rearrange("p a c -> p (a c)")[:, i * 256:(i + 1) * 256] for i in range(16)], "tb2")
            tff = tf_t[:].rearrange("p a c -> p (a c)")
            TFr = Pool([tff[:, i * 520:(i + 1) * 520] for i in range(2)], "tfr")
            TFsg = Pool([tff[:, 1040 + i * 128:1040 + (i + 1) * 128] for i in range(6)], "tfsg")
            TFkr = Pool([tff[:, 1808 + i * 256:1808 + (i + 1) * 256] for i in range(2)], "tfkr")
            TFon = Pool([tff[:, 2320 + i * 128:2320 + (i + 1) * 128] for i in range(3)], "tfon")

            def S1(i):
                h, s = its[i]
                if s == slots[0]:
                    a_ = wq.get(ahead=3)
                    b_ = wq.get(ahead=2)
                    hw[h] = (a_, b_)
                (wa, rwa), (wb, rwb) = hw[h]
                pb, rpb = PMM.next()
                xs = [X[:, kc, s * 128:(s + 1) * 128] for kc in range(KC)]
                MMg(pb[:, 0:256], [(xs[kc], wa[:, kc, :]) for kc in range(KC)], [rwa, regX[s]], rpb)
                yield
                MMg(pb[:, 256:512], [(xs[kc], wb[:, kc, :]) for kc in range(KC)], [rwb, regX[s]], rpb)
                yield
                stt[i] = dict(pb=pb, rpb=rpb)

            def S2(i):
                h, s = its[i]
                d = stt[i]
                pb, rpb = d["pb"], d["rpb"]
                qk, rqk = TB2.next()
                qk3 = qk.rearrange("p (h f) -> p h f", h=2)
                dec = [qkdec[:, 0, h:h + 1], qkdec[:, 1, h:h + 1]]
                ta, ra = TFr.next()
                A = ta[:, 0:256].rearrange("p (h c f) -> p h c f", h=2, c=2)
                Bv = ta[:, 256:512].rearrange("p (h c f) -> p h c f", h=2, c=2)
                src3 = pb[:, 0:256].rearrange("p (h f) -> p h f", h=2)
                csb = A_cs[:, s].unsqueeze(1).to_broadcast([128, 2, 2, 64])
                x1 = src3[:, :, 0:64].unsqueeze(2).to_broadcast([128, 2, 2, 64])
                x2 = src3[:, :, 64:128].unsqueeze(2).to_broadcast([128, 2, 2, 64])
                TT(A, x1, csb, ALU.mult, [rpb, rAtab], [ra])
                yield
                TT(Bv, x2, csb, ALU.mult, [rpb, rAtab, ra], [ra])
                yield
                vb, rvb = TB2.next()
                CP("act", vb[:, 0:128], pb[:, 256:384], [rpb], [rvb])
                yield
                sg, rsg = TFsg.next()
                ACT(sg[:, 0:128], pb[:, 384:512], AF.Silu, [rpb], [rsg])
                yield
                tr_, rr = TFkr.next()
                kr = tr_[:, 0:256].rearrange("p (h f) -> p h f", h=2)
                TT(kr[:, :, 0:64], A[:, :, 0, :], Bv[:, :, 1, :], ALU.subtract, [ra], [rr])
                yield
                TT(kr[:, :, 64:128], A[:, :, 1, :], Bv[:, :, 0, :], ALU.add, [ra, rr], [rr])
                yield
                for a in range(2):
                    TS(qk3[:, a, :], kr[:, a, :], dec[a], None, ALU.mult, None, [rr, rconst], [rqk])
                    yield
                pt, rpt = PTR.next()
                TR(pt[:, 0:128], qk3[:, 0, :], [rqk], rpt, mark=False)
                yield
                TR(pt[:, 128:256], qk3[:, 1, :], [rqk], rpt, mark=True)
                yield
                qkT, rqkT = TB2.next()
                CP("act", qkT[:, 0:256], pt[:, 0:256], [rpt], [rqkT])
                yield
                d.update(qk3=qk3, rqk=rqk, vb=vb, rvb=rvb, sg=sg, rsg=rsg, qkT=qkT, rqkT=rqkT)

            def S3(i):
                h, s = its[i]
                d = stt[i]
                qkT, rqkT, vb, rvb, qk3, rqk = d["qkT"], d["rqkT"], d["vb"], d["rvb"], d["qk3"], d["rqk"]
                qT = qkT[:, 0:128]
                kT = qkT[:, 128:256]
                pa, rpa = PAX.next()
                MMg(pa[:, 0:128], [(kT, qT)], [rqkT], rpa)
                stm, rstm = TB2.next()
                TT(stm[:, 0:128], pa[:, 0:128], mask01[:], ALU.mult, [rpa, rconst], [rstm])
                yield
                MMg(pa[:, 128:256], [(stm[:, 0:128], vb[:, 0:128]), (qT, Sbf[:, h, :])], [rstm, rvb, rqkT, rSbf[h]], rpa)
                yield
                MMg(pa[:, 256:384], [(qk3[:, 1, :], vb[:, 0:128])], [rqk, rvb], rpa)
                STT(Wst[:, h, :], Wst[:, h, :], gamc[:, h:h + 1], pa[:, 256:384], ALU.mult, ALU.add, [rpa, rWst[h], rconst], [rWst[h]])
                yield
                ACT(Sbf[:, h, :], Wst[:, h, :], AF.Identity, [rWst[h], rconst], [rSbf[h]], scale=gamc[:, h:h + 1])
                yield
                d.update(pa=pa, rpa=rpa)

            def S4(i):
                h, s = its[i]
                d = stt[i]
                pa, rpa, sg, rsg = d["pa"], d["rpa"], d["sg"], d["rsg"]
                sm, rsm = SM.next()
                P.op("dve", lambda e, sm=sm, pa=pa: e.bn_stats(out=sm[:, 0:6], in_=pa[:, 128:256]), reads=[rpa], writes=[rsm])
                yield
                P.op("dve", lambda e, sm=sm: e.bn_aggr(out=sm[:, 6:8], in_=sm[:, 0:6]), reads=[rsm], writes=[rsm])
                yield
                ACT(sm[:, 7:8], sm[:, 7:8], AF.Sqrt, [rsm], [rsm], bias=EPS)
                yield
                P.op("dve", lambda e, sm=sm: e.reciprocal(out=sm[:, 7:8], in_=sm[:, 7:8]), reads=[rsm], writes=[rsm])
                yield
                TS(sm[:, 8:9], sm[:, 6:7], -1.0, sm[:, 7:8], ALU.mult, ALU.mult, [rsm], [rsm])
                yield
                on, ron = TFon.next()
                ACT(on[:, 0:128], pa[:, 128:256], AF.Identity, [rpa, rsm], [ron], scale=sm[:, 7:8], bias=sm[:, 8:9])
                yield
                TT(on[:, 0:128], on[:, 0:128], A_gng[:, h * 128:(h + 1) * 128], ALU.mult, [ron, rAtab], [ron])
                yield
                orb, rorb = TB2.next()
                TT(orb[:, 0:128], on[:, 0:128], sg[:, 0:128], ALU.mult, [ron, rsg], [rorb])
                yield
                pt2, rpt2 = PTR.next()
                TR(pt2[:, 0:128], orb[:, 0:128], [rorb], rpt2, mark=True)
                yield
                CP("act", M[:, h, s * 128:(s + 1) * 128], pt2[:, 0:128], [rpt2], [regM[s]])
                yield
                conv.pump(2)
                del stt[i]

            stages = (S1, S2, S3, S4)
            for step in range(len(its) + 3):
                gens = []
                for lag, fn_ in enumerate(stages):
                    i = step - lag
                    if 0 <= i < len(its):
                        gens.append(fn_(i))
                while gens:
                    for g_ in list(gens):
                        try:
                            next(g_)
                        except StopIteration:
                            gens.remove(g_)
            P.barrier()
            chk("A1")
            rKT = Reg("A_KT")
            rV = Reg("A_V")
            cgs = colgroups(ntok)
            for kb in range(2):
                wap, wreg = wq.get()
                for j in range(2):
                    kvh = 2 * kb + j
                    lw = [wap[:, kc, j * 128:(j + 1) * 128] for kc in range(KC)]
                    if g == 0:
                        pb, rpb = PMM.next()
                        MMg(pb[:, 0:128], [(lw[kc], X[:, kc, 7 * 128:8 * 128]) for kc in range(KC)], [wreg, regX[7]], rpb)
                        CP("act", prevK[:, kvh, :], pb[:, 0:128], [rpb], [rprev])
                    for (c0, n) in cgs:
                        pb, rpb = PMM.next()
                        MMg(pb[:, 0:n], [(lw[kc], X[:, kc, c0:c0 + n]) for kc in range(KC)], [wreg] + regX[0:nt], rpb)
                        CP("act", A_KT[:, kvh, c0 // 128:(c0 + n) // 128, :], pb[:, 0:n].rearrange("p (s t) -> p s t", t=128), [rpb], [rKT])
                        wq.pump(1)
            wap, wreg = wq.get()
            for s in ([7] if g == 0 else []) + slots:
                pb, rpb = PMM.next()
                MMg(pb[:, 0:256], [(X[:, kc, s * 128:(s + 1) * 128], wap[:, kc, :]) for kc in range(KC)], [wreg, regX[s]], rpb)
                src = pb[:, 0:256].rearrange("p (h f) -> p h f", h=4).unsqueeze(2).to_broadcast([128, 4, 2, 64])
                if s == 7:
                    CP("dve", prevV[:], src, [rpb], [rprev])
                else:
                    CP("dve", A_V[:, s], src, [rpb], [rV])
                wq.pump(1)
            rqs_regs = [rAxf, Reg("A_qs1")]
            for qb in range(4):
                wap, wreg = wq.get()
                qs = A_qs[qb % 2]
                for mi in range(2):
                    for (c0, n) in cgs:
                        pb, rpb = PMM.next()
                        MMg(pb[:, 0:n], [(wap[:, kc, mi * 128:(mi + 1) * 128], X[:, kc, c0:c0 + n]) for kc in range(KC)], [wreg] + regX[0:nt], rpb)
                        CP("act", qs[:, mi, c0:c0 + n], pb[:, 0:n], [rpb], [rqs_regs[qb % 2]])
                        wq.pump(1)
                rq = rqs_regs[qb % 2]
                ptsd = {}

                def stA(s):
                    pts = []
                    for c in range(2):
                        banks = [PAX.next(), PMM.next()]
                        pt_, rpt_ = TB.next()
                        pt4 = pt_.rearrange("p (m a q) -> p m a q", m=2, a=2)
                        for a in range(2):
                            pa, rpa = banks[a]
                            for mi in range(2):
                                if c == 1:
                                    kap, kr_ = A_KT[a * 64:(a + 1) * 64, qb, s, :], rKT
                                elif s == 0:
                                    kap, kr_ = prevK[a * 64:(a + 1) * 64, qb, :], rprev
                                else:
                                    kap, kr_ = A_KT[a * 64:(a + 1) * 64, qb, s - 1, :], rKT
                                MMg(pa[:, mi * 128:(mi + 1) * 128], [(kap, qs[a * 64:(a + 1) * 64, mi, s * 128:(s + 1) * 128])], [kr_, rq], rpa)
                                yield
                        for a in range(2):
                            pa, rpa = banks[a]
                            lg, rlg = TF.next()
                            bia = A_bias[:, c, 4 * qb:4 * qb + 4, :].rearrange("p (m a) q -> p m a q", a=2)[:, :, a, :]
                            STT(lg[:, 0:256].rearrange("p (m q) -> p m q", m=2), pa[:, 0:256].rearrange("p (m q) -> p m q", m=2), 0.125, bia,
                                ALU.mult, ALU.add, [rpa, rAtab], [rlg])
                            ACT(pt4[:, :, a, :], lg[:, 0:256].rearrange("p (m q) -> p m q", m=2), AF.Exp, [rlg], [rpt_])
                            yield
                        if c == 0 and tl[s] == 0:
                            TS(pt_, pt_, flag[:, 0:1], None, ALU.mult, None, [rpt_, rconst], [rpt_])
                            yield
                        pts.append((pt_, rpt_))

                    ptsd[s] = pts

                def stB(s):
                    pts = ptsd.pop(s)
                    if s == 0:
                        vprev, rvprev = prevV[:, qb].rearrange("p a f -> p (a f)"), rprev
                    else:
                        vprev, rvprev = A_V[:, s - 1, qb].rearrange("p a f -> p (a f)"), rV
                    vcur = A_V[:, s, qb].rearrange("p a f -> p (a f)")
                    pn, rpn = PTRF.aps[0], PTRF.regs[0]
                    pd, rpd = PTRF.aps[1], PTRF.regs[1]
                    for hq in range(4):
                        cs_ = slice(hq * 128, (hq + 1) * 128)
                        MMg(pn[:, cs_], [(vprev, pts[0][0][:, cs_]), (vcur, pts[1][0][:, cs_])], [rvprev, rV, pts[0][1], pts[1][1]], rpn)
                        yield
                    for hq in range(4):
                        cs_ = slice(hq * 128, (hq + 1) * 128)
                        MMg(pd[:, cs_], [(ones[:], pts[0][0][:, cs_]), (ones[:], pts[1][0][:, cs_])], [rconst, pts[0][1], pts[1][1]], rpd)
                        yield
                    rc, rrc = TF.next()
                    for hq in range(4):
                        mi, a = hq // 2, hq % 2
                        m = 2 * qb + mi
                        ps_ = slice(a * 64, (a + 1) * 64)
                        cs_ = slice(hq * 128, (hq + 1) * 128)
                        TS(rc[ps_, cs_], pd[ps_, cs_], esink[ps_, m:m + 1], None, ALU.add, None, [rpd, rconst], [rrc])
                        yield
                        P.op("dve", lambda e, ps_=ps_, cs_=cs_, rc=rc: e.reciprocal(out=rc[ps_, cs_], in_=rc[ps_, cs_]), reads=[rrc], writes=[rrc])
                        yield
                        TT(M[ps_, 8 + m, s * 128:(s + 1) * 128], pn[ps_, cs_], rc[ps_, cs_], ALU.mult, [rpn, rrc], [regM[s]])
                        yield
                    wq.pump(1)

                for k_ in range(len(slots) + 1):
                    gens = []
                    if k_ < len(slots):
                        gens.append(stA(slots[k_]))
                    if k_ > 0:
                        gens.append(stB(slots[k_ - 1]))
                    while gens:
                        for g_ in list(gens):
                            try:
                                next(g_)
                            except StopIteration:
                                gens.remove(g_)
            CP("act", prevK[:], A_KT[:, :, nt - 1, :], [rKT, rprev], [rprev])
            CP("act", prevV[:], A_V[:, nt - 1], [rV, rprev], [rprev])
            P.barrier()
            chk("line595")

            for s in slots:
                t = xc_tile(tl[s])
                DMA(R[:, s, :], xc[t * 128:(t + 1) * 128, :], [], [regR[s]], "xr%d" % s)
            rlnt = Reg("lnt")
            load_lnt(0, rlnt)
            proj_resid(wq, 8, M, regM, slots, rlnt)
            P.barrier()
            chk("line607")
            R_to_X(slots)
            for hd in range(4):
                wa, rwa = wq.get(ahead=3)
                wb, rwb = wq.get(ahead=2)
                for (c0, n) in cgs:
                    qts = []
                    for c in range(4):
                        wap, wreg = (wa, rwa) if c < 2 else (wb, rwb)
                        pb, rpb = PMM.next()
                        MMg(pb[:, 0:n], [(wap[:, kc, (c % 2) * 128:(c % 2 + 1) * 128], X[:, kc, c0:c0 + n]) for kc in range(KC)],
                            [wreg] + regX[0:nt], rpb)
                        qt, rqt = TB.next()
                        CP("act" if c % 2 == 0 else "dve", qt[:, 0:n], pb[:, 0:n], [rpb], [rqt])
                        qts.append((qt, rqt))
                    pts = []
                    for mc in range(2):
                        pa, rpa = PAX.next()
                        MMg(pa[:, 0:n], [(memKT[:, hd * 4 + c, mc * 128:(mc + 1) * 128], qts[c][0][:, 0:n]) for c in range(4)],
                            [rmem] + [q[1] for q in qts], rpa)
                        pt_, rpt_ = TB.next()
                        ACT(pt_[:, 0:n], pa[:, 0:n], AF.Exp, [rpa], [rpt_], scale=512.0 ** -0.5)
                        pts.append((pt_, rpt_))
                    pd, rpd = PAX.next()
                    MMg(pd[:, 0:n], [(ones[:], pts[mc][0][:, 0:n]) for mc in range(2)], [rconst, pts[0][1], pts[1][1]], rpd)
                    rc, rrc = TF.next()
                    P.op("dve", lambda e, rc=rc, pd=pd, n=n: e.reciprocal(out=rc[:, 0:n], in_=pd[:, 0:n]), reads=[rpd], writes=[rrc])
                    for c in range(4):
                        pb, rpb = PMM.next()
                        MMg(pb[:, 0:n], [(memV[:, mc, hd * 512 + c * 128:hd * 512 + (c + 1) * 128], pts[mc][0][:, 0:n]) for mc in range(2)],
                            [rmem, pts[0][1], pts[1][1]], rpb)
                        TT(M[:, hd * 4 + c, c0:c0 + n], pb[:, 0:n], rc[:, 0:n], ALU.mult, [rpb, rrc], regM[c0 // 128:(c0 + n + 127) // 128])
                    wq.pump(4)
            P.barrier()
            chk("line644")
            rlnt = Reg("lnt")
            load_lnt(1, rlnt)
            proj_resid(wq, 8, M, regM, slots, rlnt)
            P.barrier()
            chk("line652")
            R_to_X(slots)
            rhid = Reg("hid")
            for fb in range(4):
                for fi in range(11):
                    f = fb * 11 + fi
                    wap, wreg = wq.get()
                    for (c0, n) in cgs:
                        bu, rbu = PMM.next()
                        MMg(bu[:, 0:n], [(wap[:, kc, 0:128], X[:, kc, c0:c0 + n]) for kc in range(KC)], [wreg] + regX[0:nt], rbu)
                        bg, rbg = PMM.next()
                        MMg(bg[:, 0:n], [(wap[:, kc, 128:256], X[:, kc, c0:c0 + n]) for kc in range(KC)], [wreg] + regX[0:nt], rbg)
                        gb, rgb = TF.next()
                        CP("dve", gb[:, 0:2], gtail[:, f, :], [rgtail], [rgb])
                        CP("act", gb[:, 2:2 + n], bg[:, 0:n], [rbg, rgb], [rgb])
                        if g == 0 and c0 == 0:
                            TS(gb[:, 128:130], gb[:, 128:130], flag[:, 0:1], None, ALU.mult, None, [rgb, rconst], [rgb])
                        gc, rgc = TF.next()
                        ACT(gc[:, 0:n], bg[:, 0:n], AF.Identity, [rbg, rconst], [rgc], scale=convp[:, f, 2:3], bias=convp[:, f, 3:4])
                        STT(gc[:, 0:n], gb[:, 1:1 + n], convp[:, f, 1:2], gc[:, 0:n], ALU.mult, ALU.add, [rgb, rgc, rconst], [rgc])
                        STT(gc[:, 0:n], gb[:, 0:n], convp[:, f, 0:1], gc[:, 0:n], ALU.mult, ALU.add, [rgb, rgc, rconst], [rgc])
                        CP("dve", gtail[:, f, :], gb[:, n:n + 2], [rgb], [rgtail])
                        ACT(gc[:, 0:n], gc[:, 0:n], AF.Silu, [rgc], [rgc])
                        TT(HID[:, fi, c0:c0 + n], gc[:, 0:n], bu[:, 0:n], ALU.mult, [rgc, rbu], [rhid])
                        wq.pump(2)
                if fb == 3:
                    rlnt3 = Reg("lnt3")
                    DMA(LNT[:, 0, :], lnp_d[4].partition_broadcast(128), [], [rlnt3] + regX[0:nt], "lnt0")
                    DMA(LNT[:, 1, :], lnp_d[5].partition_broadcast(128), [], [rlnt3] + regX[0:nt], "lnt1")
                for cb in range(8):
                    wap, wreg = wq.get()
                    for s in own:
                        pb, rpb = PMM.next()
                        MMg(pb[:, 0:256], [(HID[:, fi, s * 128:(s + 1) * 128], wap[:, fi, :]) for fi in range(11)], [wreg, rhid], rpb)
                        dst = R[:, s, cb * 256:(cb + 1) * 256]
                        if fb == 0:
                            STT(dst, dst, ALPHA, pb[:, 0:256], ALU.mult, ALU.add, [rpb, regR[s]], [regR[s]])
                        else:
                            TT(dst, dst, pb[:, 0:256], ALU.add, [rpb, regR[s]], [regR[s]])
                        if fb == 3 and cb == 7:
                            layernorm(s, rlnt3)
                            DMA(y[tl[s] * 128:(tl[s] + 1) * 128, :], R[:, s, :], [regR[s]], [], "out%d" % s)
            P.barrier()
            chk("line691")
        P.op("sp", None)
        nsem = P.emit(nc, st)
        print("build: sems=%d ops=%s" % (nsem, {e: len(P.ops[e]) for e in P.ENG}), flush=True)
    return nc


def _tile_w(W, cb=256):
    K, N = W.shape
    return np.ascontiguousarray(W.reshape(K // 128, 128, N // cb, cb).transpose(2, 1, 0, 3))


def _t5_bucket(n):
    n = np.asarray(n)
    nf = np.maximum(n, 1).astype(np.float32)
    large = 16 + (np.log(nf / np.float32(16)) / np.float32(math.log(128 / 16)) * np.float32(16)).astype(np.int32)
    large = np.minimum(large, 31)
    return np.where(n < 16, n, large)


_NC_CACHE = {}


def kernel(x, mem, w_in, ret_gn_g, swa_sinks, rel_bias, w_o, ln1_g, ln1_b,
           xa_wq, xa_wkv, xa_wo, ln2_g, ln2_b,
           ffn_w_up, ffn_conv_w, ffn_conv_b, ffn_w_down, ln3_g, ln3_b):
    f32 = np.float32
    x = np.asarray(x, f32)
    mem = np.asarray(mem, f32)
    w_in = np.asarray(w_in, f32)[0]
    B, S, _ = x.shape
    q_r, k_r, v_r, g_r = w_in[:, 0:1024], w_in[:, 1024:2048], w_in[:, 2048:3072], w_in[:, 3072:4096]
    q_s, k_s, v_s = w_in[:, 4096:5120], w_in[:, 5120:5376], w_in[:, 5376:5632]
    cols = []
    for h in range(8):
        sl = slice(h * 128, (h + 1) * 128)
        cols += [q_r[:, sl], k_r[:, sl], v_r[:, sl], g_r[:, sl]]
    w_r_t = _tile_w(np.concatenate(cols, axis=1))
    kd = []
    for kh in range(4):
        kd += [k_s[:, kh * 64:(kh + 1) * 64]] * 2
    w_s_t = _tile_w(np.concatenate([q_s] + kd + [v_s], axis=1))
    w_o_t = _tile_w(np.asarray(w_o, f32)[0])
    w_q_t = _tile_w(np.asarray(xa_wq, f32)[0])
    w_kv_t = _tile_w(np.asarray(xa_wkv, f32)[0])
    w_xo_t = _tile_w(np.asarray(xa_wo, f32)[0])
    wu = np.asarray(ffn_w_up, f32)[0]
    ucols = []
    for f in range(NF):
        ucols += [wu[:, f * 128:(f + 1) * 128], wu[:, FF + f * 128:FF + (f + 1) * 128]]
    w_up_t = _tile_w(np.concatenate(ucols, axis=1))
    wd = np.asarray(ffn_w_down, f32)[0]
    w_dn_t = np.ascontiguousarray(wd.reshape(4, 11, 128, 8, 256).transpose(0, 3, 2, 1, 4))
    sinks = np.asarray(swa_sinks, f32)[0]
    p = np.arange(128)
    sinkrep = np.stack([sinks[2 * m + (p >= 64)] for m in range(8)], axis=1).astype(f32)
    rb = np.asarray(rel_bias, f32)
    jj = np.arange(128)[:, None]
    ii = np.arange(128)[None, :]
    biasT = np.full((128, 2, 16, 128), NEG, f32)
    d_prev = ii + 128 - jj
    d_cur = ii - jj
    bp = rb[_t5_bucket(np.clip(d_prev, 0, 127))]
    bc = rb[_t5_bucket(np.clip(d_cur, 0, 127))]
    vp = (d_prev < 128)
    vc = (d_cur >= 0)
    biasT[:, 0] = np.where(vp[:, None, :], bp.transpose(0, 2, 1), NEG)
    biasT[:, 1] = np.where(vc[:, None, :], bc.transpose(0, 2, 1), NEG)
    lnp = np.stack([np.asarray(a, f32)[0] for a in (ln1_g, ln1_b, ln2_g, ln2_b, ln3_g, ln3_b)])
    cw = np.asarray(ffn_conv_w, f32)[0]
    cbv = np.asarray(ffn_conv_b, f32)[0]
    convp = np.zeros((128, NF, 4), f32)
    for t in range(3):
        convp[:, :, t] = cw[t].reshape(NF, 128).T
    convp[:, :, 3] = cbv.reshape(NF, 128).T
    gng = np.asarray(ret_gn_g, f32)[0]
    lg = np.log1p(-np.exp2(-5.0 - np.arange(8, dtype=np.float64)))
    pp = np.arange(128, dtype=np.float64)
    qkdec = np.zeros((128, 2, 8), np.float64)
    qkdec[:, 0, :] = np.exp(lg[None, :] * (pp[:, None] + 1.0))
    qkdec[:, 1, :] = np.exp(-lg[None, :] * (pp[:, None] + 1.0)) * (128.0 ** -0.5)
    gamc = np.tile(np.exp(lg * 128.0)[None, :], (128, 1))
    jv = np.arange(NPRE, dtype=np.float64)
    expo = (NPRE * 128 - 1 - 128.0) - 128.0 * jv[None, :, None] - pp[:, None, None]
    kdec_pre = np.exp(lg[None, None, :] * expo) * (128.0 ** -0.5)
    mask01 = (jj <= ii).astype(f32)
    ident = np.eye(128, dtype=f32)
    inv = (1.0 / (10000.0 ** (np.arange(64, dtype=f32) / f32(64)))).astype(f32)

    shared = dict(w_r=w_r_t, w_s=w_s_t, w_o=w_o_t, w_q=w_q_t, w_kv=w_kv_t, w_xo=w_xo_t, w_up=w_up_t, w_dn=w_dn_t,
                  kdec_pre=kdec_pre.astype(f32), qkdec=qkdec.astype(f32), mask01=mask01, ident=ident, biasT=biasT,
                  sinkrep=sinkrep, gng=gng, lnp=lnp, convp=convp, gamc=gamc.astype(f32))
    in_maps = []
    per = S // 4
    for c in range(8):
        b, q = c // 4, c % 4
        start = q * per
        lo = start - (NPRE + 1) * 128
        xcc = np.zeros((nxc() * 128, D), f32)
        src_lo = max(lo, 0)
        xcc[src_lo - lo:] = x[b, src_lo:start + per]
        pos = (lo + np.arange(nxc() * 128)).astype(f32)
        ang = pos[:, None] * inv[None, :]
        cs = np.stack([np.cos(ang), np.sin(ang)], axis=1).astype(f32)
        cs = cs.reshape(nxc(), 128, 2, 64).transpose(1, 0, 2, 3)
        m = dict(shared)
        m.update(xc=xcc, memb=np.ascontiguousarray(mem[b]),
                 cs_pre=np.ascontiguousarray(cs[:, :NPRE]), cs_main=np.ascontiguousarray(cs[:, NPRE:]),
                 flag=np.full((128, 1), 0.0 if q == 0 else 1.0, f32))
        in_maps.append(m)
    if "nc" not in _NC_CACHE:
        _NC_CACHE["nc"] = build()
    res = run_bass_kernel_spmd(_NC_CACHE["nc"], in_maps, core_ids=list(range(8)))
    out = np.zeros((B, S, D), f32)
    for c in range(8):
        b, q = c // 4, c % 4
        out[b, q * per:(q + 1) * per] = res.results[c]["y"]
    return out
```
